# Optimizing a Trainium2 kernel written in Bass

```python
import math
import jax, jax.numpy as jnp
from jax import lax
import numpy as np

D_MODEL = 1024
BATCH = 4
SEQ = 8192
DEPTH = 1

HEAD_DIM = D_MODEL // 16
N_ATTN_HEADS = 12
N_GMLP_GROUPS = 4
GMLP_DIM = HEAD_DIM
ATTN_WIDTH = N_ATTN_HEADS * HEAD_DIM
GMLP_WIDTH = N_GMLP_GROUPS * GMLP_DIM
MIX_WIDTH = ATTN_WIDTH + GMLP_WIDTH
IN_WIDTH = 3 * ATTN_WIDTH + 2 * GMLP_WIDTH
CHUNK = 128
BLOCK = 128
DILATED_PATTERNS = ((128, 1), (512, 4), (2048, 16))
D_FF = 4 * D_MODEL
EPS = 1e-6

kernel_name = "hymba_gmlp_longnet_alibi_block"


def alibi_slopes(n):
    def pow2_slopes(m):
        start = 2.0 ** (-8.0 / m)
        return [start ** (i + 1) for i in range(m)]
    if math.log2(n).is_integer():
        s = pow2_slopes(n)
    else:
        c = 2 ** int(math.floor(math.log2(n)))
        s = pow2_slopes(c) + pow2_slopes(2 * c)[0::2][: n - c]
    return np.asarray(s, dtype=np.float32)


def rms_norm(x, g):
    xf = x.astype(jnp.float32)
    y = xf * lax.rsqrt(jnp.mean(xf * xf, axis=-1, keepdims=True) + EPS)
    return (y * g.astype(jnp.float32)).astype(x.dtype)


def layer_norm(x, g, b):
    xf = x.astype(jnp.float32)
    mu = jnp.mean(xf, axis=-1, keepdims=True)
    var = jnp.mean(jnp.square(xf - mu), axis=-1, keepdims=True)
    y = (xf - mu) * lax.rsqrt(var + EPS)
    return (y * g.astype(jnp.float32) + b.astype(jnp.float32)).astype(x.dtype)


def chunked_spatial_gating(u, z, ln_g, ln_b, w_s, b_s):
    B, S, G, C = u.shape
    u = jax.nn.gelu(u)
    z = layer_norm(jax.nn.gelu(z), ln_g, ln_b)
    zc = z.reshape(B, S // CHUNK, CHUNK, G, C)
    causal = jnp.tril(jnp.ones((CHUNK, CHUNK), dtype=w_s.dtype))
    ws = w_s * causal[None]
    mixed = jnp.einsum('gts,bnsgc->bntgc', ws, zc) + b_s.T[None, None, :, :, None]
    return u * mixed.reshape(B, S, G, C)


def dilated_window_attention(q, k, v, slopes, window, dilation):
    B, S, H, Dh = q.shape
    span = BLOCK * dilation
    S_pad = -(-S // span) * span
    pad = S_pad - S
    L = S_pad // dilation
    nb = L // BLOCK

    def to_sub(t):
        t = jnp.pad(t.astype(jnp.float32), ((0, 0), (0, pad), (0, 0), (0, 0)))
        t = t.reshape(B, L, dilation, H, Dh).transpose(0, 2, 3, 1, 4)
        return t.reshape(B, dilation, H, nb, BLOCK, Dh)

    qs, ks, vs = to_sub(q), to_sub(k), to_sub(v)
    blk_pad = ((0, 0), (0, 0), (0, 0), (1, 0), (0, 0), (0, 0))
    kb = jnp.concatenate([jnp.pad(ks, blk_pad)[:, :, :, :-1], ks], axis=4)
    vb = jnp.concatenate([jnp.pad(vs, blk_pad)[:, :, :, :-1], vs], axis=4)

    scores = jnp.einsum('brhnqd,brhnkd->brhnqk', qs, kb)
    qi = jnp.arange(BLOCK)[:, None]
    kj = jnp.arange(2 * BLOCK)[None, :]
    steps = qi + BLOCK - kj
    band = (steps >= 0) & (steps <= window // dilation)
    blk = jnp.arange(nb)[:, None, None]
    valid = band[None] & ~((blk == 0) & (kj[None] < BLOCK))
    alibi = -slopes[:, None, None] * (steps * dilation).astype(jnp.float32)[None]
    scores = scores + alibi[None, None, :, None]
    scores = jnp.where(valid[None, None, None], scores, -jnp.inf)

    m = jnp.max(scores, axis=-1, keepdims=True)
    p = jnp.exp(scores - m)
    l = jnp.sum(p, axis=-1, keepdims=True)
    o = jnp.einsum('brhnqk,brhnkd->brhnqd', p, vb) / l
    lse = (m + jnp.log(l))[..., 0]

    o = o.reshape(B, dilation, H, L, Dh).transpose(0, 3, 1, 2, 4).reshape(B, S_pad, H, Dh)[:, :S]
    lse = lse.reshape(B, dilation, H, L).transpose(0, 3, 1, 2).reshape(B, S_pad, H)[:, :S]
    return o, lse


def mixture_of_dilations(q, k, v, slopes):
    outs, lses = [], []
    for window, dilation in DILATED_PATTERNS:
        o, lse = dilated_window_attention(q, k, v, slopes, window, dilation)
        outs.append(o)
        lses.append(lse)
    w = jax.nn.softmax(jnp.stack(lses, axis=0), axis=0)
    return jnp.sum(w[..., None] * jnp.stack(outs, axis=0), axis=0)


def setup_inputs(seed: int = 0) -> dict:
    key = jax.random.key(seed)
    ks = jax.random.split(key, 16)
    f32 = jnp.float32
    nrm = lambda k, shape, scale: jax.random.normal(k, shape, f32) * scale
    G, C = N_GMLP_GROUPS, GMLP_DIM
    return {
        "x": nrm(ks[0], (BATCH, SEQ, D_MODEL), 1.0),
        "norm1_g": 1.0 + nrm(ks[1], (DEPTH, D_MODEL), 0.02),
        "w_in": nrm(ks[2], (DEPTH, D_MODEL, IN_WIDTH), D_MODEL ** -0.5),
        "sgu_ln_g": 1.0 + nrm(ks[3], (DEPTH, G, C), 0.02),
        "sgu_ln_b": nrm(ks[4], (DEPTH, G, C), 0.02),
        "sgu_w": nrm(ks[5], (DEPTH, G, CHUNK, CHUNK), CHUNK ** -0.5),
        "sgu_b": 1.0 + nrm(ks[6], (DEPTH, G, CHUNK), 0.02),
        "attn_out_g": 1.0 + nrm(ks[7], (DEPTH, ATTN_WIDTH), 0.02),
        "gmlp_out_g": 1.0 + nrm(ks[8], (DEPTH, GMLP_WIDTH), 0.02),
        "w_out": nrm(ks[9], (DEPTH, MIX_WIDTH, D_MODEL), MIX_WIDTH ** -0.5),
        "norm2_g": 1.0 + nrm(ks[10], (DEPTH, D_MODEL), 0.02),
        "w_ff1": nrm(ks[11], (DEPTH, D_MODEL, D_FF), D_MODEL ** -0.5),
        "w_ff2": nrm(ks[12], (DEPTH, D_FF, D_MODEL), D_FF ** -0.5),
        "final_norm_g": 1.0 + nrm(ks[13], (D_MODEL,), 0.02),
    }


def reference(x, norm1_g, w_in, sgu_ln_g, sgu_ln_b, sgu_w, sgu_b, attn_out_g, gmlp_out_g,
              w_out, norm2_g, w_ff1, w_ff2, final_norm_g):
    B, S, _ = x.shape
    slopes = jnp.asarray(alibi_slopes(N_ATTN_HEADS), dtype=jnp.float32)
    scale = HEAD_DIM ** -0.5
    A, Gw = ATTN_WIDTH, GMLP_WIDTH
    h = x
    for l in range(DEPTH):
        hn = rms_norm(h, norm1_g[l])
        proj = hn @ w_in[l]
        q = (proj[..., :A] * scale).reshape(B, S, N_ATTN_HEADS, HEAD_DIM)
        k = proj[..., A:2 * A].reshape(B, S, N_ATTN_HEADS, HEAD_DIM)
        v = proj[..., 2 * A:3 * A].reshape(B, S, N_ATTN_HEADS, HEAD_DIM)
        u = proj[..., 3 * A:3 * A + Gw].reshape(B, S, N_GMLP_GROUPS, GMLP_DIM)
        z = proj[..., 3 * A + Gw:].reshape(B, S, N_GMLP_GROUPS, GMLP_DIM)

        attn = mixture_of_dilations(q, k, v, slopes).astype(h.dtype).reshape(B, S, A)
        gmlp = chunked_spatial_gating(u, z, sgu_ln_g[l], sgu_ln_b[l], sgu_w[l], sgu_b[l]).reshape(B, S, Gw)

        mixed = jnp.concatenate([rms_norm(attn, attn_out_g[l]), rms_norm(gmlp, gmlp_out_g[l])], axis=-1)
        h = h + mixed @ w_out[l]

        hn = rms_norm(h, norm2_g[l])
        h = h + jnp.square(jax.nn.relu(hn @ w_ff1[l])) @ w_ff2[l]
    return rms_norm(h, final_norm_g)
```

```python
import math
import contextlib
import numpy as np
import concourse.bass as bass
import concourse.mybir as mybir
from concourse.bass_utils import run_bass_kernel_spmd

F32 = mybir.dt.float32
BF16 = mybir.dt.bfloat16
AF = mybir.ActivationFunctionType
ALU = mybir.AluOpType
AX = mybir.AxisListType
EPS = 1e-6
PATTERNS = (1, 4, 16)
N_CORES = 8
DBG = dict(spans=2, pairs=6, phaseC=True, gmlp=True, tilesC=4)


class Sched:
    ENGS = ("pe", "act", "dve", "pool", "sp")

    def __init__(self, nc):
        self.nc = nc
        self.eng = {"pe": nc.tensor, "act": nc.scalar, "dve": nc.vector, "pool": nc.gpsimd, "sp": nc.sync}
        self.ops = []
        self.last_w = {}
        self.readers = {}
        self.last_by_eng = {}
        self.dma_since_barrier = []
        self.pending_barrier = {}

    def barrier(self):
        deps = set(self.last_by_eng.values()) | set(self.dma_since_barrier)
        self.dma_since_barrier = []
        for e in self.ENGS:
            self.pending_barrier.setdefault(e, set()).update(deps)

    def op(self, eng, fn, reads=(), writes=(), dma=False):
        idx = len(self.ops)
        deps = set()
        raw = set()
        for k in reads:
            w = self.last_w.get(k)
            if w is not None:
                deps.add(w)
                raw.add(w)
        for k in writes:
            w = self.last_w.get(k)
            if w is not None:
                deps.add(w)
            for r in self.readers.get(k, {}).values():
                deps.add(r)
        pb = self.pending_barrier.pop(eng, None)
        if pb:
            deps |= pb
            raw |= pb
        deps.discard(idx)
        self.ops.append(dict(eng=eng, fn=fn, deps=deps, raw=raw, dma=dma))
        for k in reads:
            rd = self.readers.setdefault(k, {})
            if dma:
                rd[("dma", idx)] = idx
            else:
                rd[eng] = idx
        for k in writes:
            self.last_w[k] = idx
            self.readers[k] = {}
        if dma:
            self.dma_since_barrier.append(idx)
        else:
            self.last_by_eng[eng] = idx
        return idx

    def _needs_sem(self, o, d, od):
        if od["dma"]:
            return True
        if od["eng"] != o["eng"]:
            return True
        if o["dma"]:
            return True
        return o["eng"] != "pe"

    def emit(self, final_wait_ops=()):
        nc = self.nc
        ops = self.ops
        n = len(ops)
        need = [False] * n
        for o in ops:
            for d in o["deps"]:
                if self._needs_sem(o, d, ops[d]):
                    need[d] = True
        for d in final_wait_ops:
            need[d] = True
        with contextlib.ExitStack() as st:
            esem = {e: st.enter_context(nc.semaphore("s_" + e)) for e in self.ENGS}
            NDMA = 40
            dsem = [st.enter_context(nc.semaphore("d%d" % i)) for i in range(NDMA)]
            slots_of = {"sp": list(range(0, 24)), "pool": list(range(24, 40))}
            slot_rr = {"sp": 0, "pool": 0}
            ecount = {e: 0 for e in self.ENGS}
            dcount = [0] * NDMA
            dnext = 0
            sig = [None] * n
            waited = {e: {} for e in self.ENGS}
            nwaits = 0

            def do_wait(e, s):
                nonlocal nwaits
                sem, val, key = s
                if waited[e].get(key, -1) >= val:
                    return
                self.eng[e].wait_ge(sem, val)
                waited[e][key] = val
                nwaits += 1

            for i, o in enumerate(ops):
                e = o["eng"]
                for d in sorted(o["deps"]):
                    if need[d] and self._needs_sem(o, d, ops[d]):
                        do_wait(e, sig[d])
                if o["dma"]:
                    slot = slots_of[e][slot_rr[e] % len(slots_of[e])]
                    slot_rr[e] += 1
                    dnext += 1
                    if dcount[slot] > 0:
                        do_wait(e, (dsem[slot], dcount[slot], ("d", slot)))
                    inst = o["fn"](self.eng[e])
                    dcount[slot] += 16
                    inst.then_inc(dsem[slot], 16)
                    sig[i] = (dsem[slot], dcount[slot], ("d", slot))
                else:
                    inst = o["fn"](self.eng[e])
                    if need[i]:
                        ecount[e] += 1
                        inst.then_inc(esem[e], 1)
                        sig[i] = (esem[e], ecount[e], ("e", e))
            for d in final_wait_ops:
                do_wait("sp", sig[d])
            self.stats = dict(n_ops=n, ecount=dict(ecount), ndma=dnext, nwaits=nwaits)


def alibi_slopes(n):
    def pow2_slopes(m):
        start = 2.0 ** (-8.0 / m)
        return [start ** (i + 1) for i in range(m)]
    if math.log2(n).is_integer():
        s = pow2_slopes(n)
    else:
        c = 2 ** int(math.floor(math.log2(n)))
        s = pow2_slopes(c) + pow2_slopes(2 * c)[0::2][: n - c]
    return np.asarray(s, dtype=np.float32)


def make_wtab():
    sl = alibi_slopes(12).astype(np.float64)
    i = np.arange(128)[:, None]
    j = np.arange(128)[None, :]
    tab = np.zeros((128, 12, 3, 256), dtype=np.float64)
    for h in range(12):
        for pi, d in enumerate(PATTERNS):
            steps_prev = j + 128 - i
            steps_cur = j - i
            tab[:, h, pi, 0:128] = np.where(j <= i, np.exp(-sl[h] * d * np.maximum(steps_prev, 0)), 0.0)
            tab[:, h, pi, 128:256] = np.where(j >= i, np.exp(-sl[h] * d * np.maximum(steps_cur, 0)), 0.0)
    return tab.astype(np.float32)


def vid(d, r, b):
    hb = (32 // d) // 2
    return (b // hb) * 16 + r * hb + (b % hb)


def vid_inv(d, i):
    hb = (32 // d) // 2
    sec, w = divmod(i, 16)
    r, bb = divmod(w, hb)
    return r, sec * hb + bb


def build_program():
    nc = bass.Bass("TRN2", target_bir_lowering=False)

    def din(name, shape):
        return nc.dram_tensor(name, shape, F32, kind="ExternalInput").ap()

    x_own = din("x_own", [4096, 1024])
    x_halo = din("x_halo", [2048, 1024])
    hmask_d = din("hmask", [128, 64])
    w_in = din("w_in", [1024, 2816])
    w_out = din("w_out", [1024, 1024])
    w_ff1 = din("w_ff1", [1024, 4096])
    w_ff2 = din("w_ff2", [4096, 1024])
    g1_d = din("g1_bc", [128, 1024])
    g2_d = din("g2_bc", [128, 1024])
    gf_d = din("gf_bc", [128, 1024])
    ga_d = din("ga_col", [128, 6])
    gg_d = din("gg_bc", [128, 256])
    lng_d = din("lng_bc", [128, 256])
    lnb_d = din("lnb_bc", [128, 256])
    bs_d = din("bsT", [128, 4])
    ws_d = din("wsT", [128, 4, 128])
    tri_d = din("tri", [128, 128])
    wtab_d = din("wtab", [128, 12, 3, 256])
    ident_d = din("ident", [128, 128])
    out_d = nc.dram_tensor("out", [4096, 1024], F32, kind="ExternalOutput").ap()
    w1s = nc.dram_tensor("w1s", [128, 8, 8, 512], BF16, kind="Internal").ap()
    w2s = nc.dram_tensor("w2s", [128, 2, 8, 4, 512], BF16, kind="Internal").ap()
    kvs = nc.dram_tensor("kvs", [6, 2, 128, 2048], BF16, kind="Internal").ap()

    w_in_v = w_in.rearrange("(c p) n -> p c n", p=128)
    w_out_v = w_out.rearrange("(c p) n -> p c n", p=128)
    w_ff1_v = w_ff1.rearrange("(c p) n -> p c n", p=128)
    w_ff2_v = w_ff2.rearrange("(c p) n -> p c n", p=128)

    S = Sched(nc)
    out_ops = []
    uid = [0]

    def alloc(st, name, shape, dt):
        uid[0] += 1
        return st.enter_context(nc.sbuf_tensor("%s_%d" % (name, uid[0]), shape, dt))

    def dma(q, out, in_, reads, writes):
        return S.op(q, lambda e, out=out, in_=in_: e.dma_start(out=out, in_=in_), reads=reads, writes=writes, dma=True)

    def mm(out, lhsT, rhs, start, stop, reads, writes):
        S.op("pe", lambda e, out=out, lhsT=lhsT, rhs=rhs, start=start, stop=stop:
             e.matmul(out, lhsT=lhsT, rhs=rhs, start=start, stop=stop), reads=reads, writes=writes)

    def tr(out, in_, ident, reads, writes):
        S.op("pe", lambda e, out=out, in_=in_, ident=ident: e.transpose(out, in_, ident), reads=reads, writes=writes)

    def act(out, in_, func, reads, writes, **kw):
        S.op("act", lambda e, out=out, in_=in_, func=func, kw=kw: e.activation(out=out, in_=in_, func=func, **kw),
             reads=reads, writes=writes)

    def tt(eng, out, in0, in1, op, reads, writes):
        S.op(eng, lambda e, out=out, in0=in0, in1=in1, op=op: e.tensor_tensor(out=out, in0=in0, in1=in1, op=op),
             reads=reads, writes=writes)

    def stt(eng, out, in0, scalar, in1, op0, op1, reads, writes):
        S.op(eng, lambda e, out=out, in0=in0, scalar=scalar, in1=in1, op0=op0, op1=op1:
             e.scalar_tensor_tensor(out=out, in0=in0, scalar=scalar, in1=in1, op0=op0, op1=op1), reads=reads, writes=writes)

    def ts1(eng, out, in0, scalar, op, reads, writes):
        S.op(eng, lambda e, out=out, in0=in0, scalar=scalar, op=op:
             e.tensor_scalar(out=out, in0=in0, scalar1=scalar, scalar2=None, op0=op), reads=reads, writes=writes)

    def cp(eng, out, in_, reads, writes):
        if eng == "act":
            S.op("act", lambda e, out=out, in_=in_: e.copy(out=out, in_=in_), reads=reads, writes=writes)
        else:
            S.op(eng, lambda e, out=out, in_=in_: e.tensor_copy(out=out, in_=in_), reads=reads, writes=writes)

    def memset(eng, ap, val, writes):
        S.op(eng, lambda e, ap=ap, val=val: e.memset(ap, val), writes=writes)

    def rstd_chain(ss_ap, ln_ap, rs_ap, n, kss, kln, krs):
        act(ln_ap, ss_ap, AF.Ln, [kss], [kln], scale=1.0 / n, bias=EPS)
        act(rs_ap, ln_ap, AF.Exp, [kln], [krs], scale=-0.5)

    evac_rr = [0]

    def evac_eng():
        evac_rr[0] += 1
        return "act" if evac_rr[0] % 2 else "dve"

    with contextlib.ExitStack() as G:
        psF = G.enter_context(nc.psum_tensor("ps8", [128, 8, 512], F32))
        psT = psF[:, 6:8, :].bitcast(BF16)
        ident = alloc(G, "ident", [128, 128], BF16)
        hm = alloc(G, "hm", [128, 64], F32)
        gacol = alloc(G, "gacol", [128, 6], F32)
        onesb = alloc(G, "onesb", [128, 1], BF16)
        dma("pool", ident[:], ident_d, [], [("ident",)])
        dma("sp", hm[:], hmask_d, [], [("hm",)])
        dma("sp", gacol[:], ga_d, [], [("gacol",)])
        memset("pool", onesb[:], 1.0, [("onesb",)])

        def pTkeys(tb):
            return [("pf", 6 + tb)]

        def precast_ffn(part):
            jobs = [("w1", g) for g in range(8)] + [("w2", h, g) for h in range(2) for g in range(8)]
            for job in jobs[part * 6:(part + 1) * 6]:
                if job[0] == "w1":
                    g = job[1]
                    dma("pool", w1s[:, g], w_ff1_v[:, :, g * 512:(g + 1) * 512], [], [("w1s", g)])
                else:
                    _, h, g = job
                    dma("pool", w2s[:, h, g], w_ff2_v[:, 4 * g:4 * g + 4, h * 512:(h + 1) * 512], [], [("w2s", h, g)])

        for s in range(DBG['spans']):
            with contextlib.ExitStack() as SP:
                attnT = alloc(SP, "attnT", [128, 6, 2048], BF16)
                gmT = alloc(SP, "gmT", [128, 2, 2048], BF16)
                wo = alloc(SP, "wo", [128, 8, 1024], BF16)
                with contextlib.ExitStack() as SAB:
                    hnT = alloc(SAB, "hnT", [128, 8, 4096], BF16)
                    with contextlib.ExitStack() as SA:
                        NXB = 3
                        g1bc = alloc(SA, "g1bc", [128, 1024], F32)
                        xt = [alloc(SA, "xt%d" % i, [128, 2, 1024], F32) for i in range(NXB)]
                        xs = [alloc(SA, "xs%d" % i, [128, 1024], BF16) for i in range(4)]
                        junk = alloc(SA, "junkA", [128, 1024], BF16)
                        ssA = alloc(SA, "ssA", [128, 32], F32)
                        lnA = alloc(SA, "lnA", [128, 32], F32)
                        rsA = alloc(SA, "rsA", [128, 32], F32)
                        dma("sp", g1bc[:], g1_d, [], [("g1bc",)])
                        xctr = [0]
                        xbuf_of = {}

                        def A1(kb):
                            if kb % 2 == 0:
                                if kb < 16:
                                    src = (x_halo if s == 0 else x_own)[kb * 128:(kb + 2) * 128, :]
                                else:
                                    r0 = s * 2048 + (kb - 16) * 128
                                    src = x_own[r0:r0 + 256, :]
                                bq = xctr[0] % NXB
                                xctr[0] += 1
                                xbuf_of[kb] = bq
                                xbuf_of[kb + 1] = bq
                                dma("sp", xt[bq][:], src.rearrange("(j p) f -> p j f", p=128), [], [("xt", bq, 0), ("xt", bq, 1)])
                            bq = xbuf_of[kb]
                            j = kb % 2
                            b = kb % 4
                            act(junk[:], xt[bq][:, j, :], AF.Square, [("xt", bq, j)], [("junkA",), ("ssA", kb)],
                                accum_out=ssA[:, kb:kb + 1])
                            rstd_chain(ssA[:, kb:kb + 1], lnA[:, kb:kb + 1], rsA[:, kb:kb + 1], 1024.0,
                                       ("ssA", kb), ("lnA", kb), ("rsA", kb))
                            stt("dve", xs[b][:], xt[bq][:, j, :], rsA[:, kb:kb + 1], g1bc[:], ALU.mult, ALU.mult,
                                [("xt", bq, j), ("rsA", kb), ("g1bc",)], [("xs", b)])

                        def A2(kb):
                            b = kb % 4
                            tb = kb % 2
                            for c in range(8):
                                tr(psT[:, tb, c * 128:(c + 1) * 128], xs[b][:, c * 128:(c + 1) * 128], ident[:],
                                   [("xs", b), ("ident",)], pTkeys(tb))
                            cp(evac_eng(), hnT[:, :, kb * 128:(kb + 1) * 128],
                               psT[:, tb, :].rearrange("p (c i) -> p c i", i=128), pTkeys(tb), [("hnT", kb)])

                        SKA = 2
                        kbs = list(range(32)) if s == 0 else list(range(16, 32))
                        for ii in range(len(kbs) + SKA):
                            if ii < len(kbs):
                                A1(kbs[ii])
                            if ii - SKA >= 0:
                                A2(kbs[ii - SKA])
                    S.barrier()
                    with contextlib.ExitStack() as SG:
                        wuz = alloc(SG, "wuz", [128, 8, 512], BF16)
                        wsT = alloc(SG, "wsT", [128, 4, 128], BF16)
                        wsf = alloc(SG, "wsf", [128, 4, 128], F32)
                        tri = alloc(SG, "tri", [128, 128], F32)
                        lng = alloc(SG, "lng", [128, 256], F32)
                        lnb = alloc(SG, "lnb", [128, 256], F32)
                        ggb = alloc(SG, "ggb", [128, 256], F32)
                        bsT = alloc(SG, "bsT", [128, 4], F32)
                        guzU = alloc(SG, "guzU", [128, 16, 256], F32)
                        guzZ = alloc(SG, "guzZ", [128, 16, 256], F32)
                        NB = 2
                        NSET = 4
                        NBT = (16 // NB) if DBG['gmlp'] else 0
                        sqb_ = [alloc(SG, "sqb%d" % i, [128, NB, 256], F32) for i in range(NSET)]
                        zc_ = [alloc(SG, "zc%d" % i, [128, NB, 256], F32) for i in range(NSET)]
                        zc2_ = [alloc(SG, "zc2%d" % i, [128, NB, 256], F32) for i in range(NSET)]
                        mb_ = [alloc(SG, "mb%d" % i, [128, NB, 256], F32) for i in range(NSET)]
                        zn_ = [alloc(SG, "zn%d" % i, [128, NB, 256], BF16) for i in range(NSET)]
                        gmn_ = [alloc(SG, "gmn%d" % i, [128, NB, 256], BF16) for i in range(NSET)]
                        msum = alloc(SG, "msum", [128, 8, 8], F32)
                        qsum = alloc(SG, "qsum", [128, 8, 8], F32)
                        mean = alloc(SG, "mean", [128, 8, 8], F32)
                        m2 = alloc(SG, "m2", [128, 8, 8], F32)
                        var = alloc(SG, "var", [128, 8, 8], F32)
                        lnv = alloc(SG, "lnv", [128, 8, 8], F32)
                        rsl = alloc(SG, "rsl", [128, 8, 8], F32)
                        ssg = alloc(SG, "ssg", [128, 8, 2], F32)
                        lgg = alloc(SG, "lgg", [128, 8, 2], F32)
                        rg = alloc(SG, "rg", [128, 8, 2], F32)
                        dma("pool", wuz[:], w_in_v[:, :, 2304:2816], [], [("wuz",)])
                        dma("sp", wsf[:], ws_d, [], [("wsf",)])
                        dma("sp", tri[:], tri_d, [], [("tri",)])
                        dma("sp", lng[:], lng_d, [], [("lng",)])
                        dma("sp", lnb[:], lnb_d, [], [("lnb",)])
                        dma("sp", ggb[:], gg_d, [], [("ggb",)])
                        dma("sp", bsT[:], bs_d, [], [("bsT",)])
                        tt("dve", wsT[:], wsf[:], tri[:].unsqueeze(1).broadcast_to([128, 4, 128]), ALU.mult,
                           [("wsf",), ("tri",)], [("wsT",)])
                        for i in range(16 if DBG['gmlp'] else 0):
                            kb = 16 + i
                            bank = 4 + i % 2
                            for c in range(8):
                                mm(psF[:, bank, :], hnT[:, c, kb * 128:(kb + 1) * 128], wuz[:, c, :], c == 0, c == 7,
                                   [("hnT", kb), ("wuz",)], [("pf", bank)])
                            act(guzU[:, i, :], psF[:, bank, 0:256], AF.Gelu_apprx_tanh, [("pf", bank)], [("guzU", i // 8)])
                            act(guzZ[:, i, :], psF[:, bank, 256:512], AF.Gelu_apprx_tanh, [("pf", bank)], [("guzZ", i // 8)])
                        def gm_batch(bt):
                            L = []
                            i0 = bt * NB
                            st_ = bt % NSET
                            sqb, zc, zc2, mb, zn, gmn = sqb_[st_], zc_[st_], zc2_[st_], mb_[st_], zn_[st_], gmn_[st_]
                            K = lambda n: (n, st_)
                            zfl = guzZ[:, i0:i0 + NB, :]
                            zv = zfl.rearrange("p b (g c) -> p (b g) c", c=64)
                            kz = [("guzZ", i0 // 8)]
                            ku = [("guzU", i0 // 8)]
                            zc3 = zc[:].rearrange("p b (g c) -> p (b g) c", c=64)
                            zc23 = zc2[:].rearrange("p b (g c) -> p (b g) c", c=64)
                            mbk = st_
                            tb = bt % 2
                            toff = ((bt // 2) % 2) * 512
                            L.append(lambda: S.op("dve", lambda e, o=msum[:, bt, :], a=zv: e.tensor_reduce(out=o, in_=a, axis=AX.X, op=ALU.add),
                                                  reads=kz, writes=[("msum", bt)]))
                            L.append(lambda: act(sqb[:], zfl, AF.Square, kz, [K("sqb")]))
                            L.append(lambda: S.op("dve", lambda e, o=qsum[:, bt, :], a=sqb[:].rearrange("p b (g c) -> p (b g) c", c=64):
                                                  e.tensor_reduce(out=o, in_=a, axis=AX.X, op=ALU.add), reads=[K("sqb")], writes=[("qsum", bt)]))
                            L.append(lambda: ts1("dve", mean[:, bt, :], msum[:, bt, :], 1.0 / 64, ALU.mult, [("msum", bt)], [("mean", bt)]))
                            L.append(lambda: tt("dve", m2[:, bt, :], mean[:, bt, :], mean[:, bt, :], ALU.mult, [("mean", bt)], [("m2", bt)]))
                            L.append(lambda: stt("dve", var[:, bt, :], qsum[:, bt, :], 1.0 / 64, m2[:, bt, :], ALU.mult, ALU.subtract,
                                                 [("qsum", bt), ("m2", bt)], [("var", bt)]))
                            L.append(lambda: act(lnv[:, bt, :], var[:, bt, :], AF.Ln, [("var", bt)], [("lnv", bt)], bias=EPS))
                            L.append(lambda: act(rsl[:, bt, :], lnv[:, bt, :], AF.Exp, [("lnv", bt)], [("rsl", bt)], scale=-0.5))
                            L.append(lambda: tt("dve", zc3, zv, mean[:, bt, :].unsqueeze(2).broadcast_to([128, 4 * NB, 64]), ALU.subtract,
                                                kz + [("mean", bt)], [K("zc")]))
                            L.append(lambda: tt("dve", zc23, zc3, rsl[:, bt, :].unsqueeze(2).broadcast_to([128, 4 * NB, 64]), ALU.mult,
                                                [K("zc"), ("rsl", bt)], [K("zc2")]))
                            L.append(lambda: tt("dve", zc[:], zc2[:], lng[:, :].unsqueeze(1).broadcast_to([128, NB, 256]), ALU.mult,
                                                [K("zc2"), ("lng",)], [K("zc")]))
                            L.append(lambda: tt("dve", zn[:], zc[:], lnb[:, :].unsqueeze(1).broadcast_to([128, NB, 256]), ALU.add,
                                                [K("zc"), ("lnb",)], [K("zn")]))

                            def mix_mms():
                                for j in range(NB):
                                    bk = mbk
                                    for g in range(4):
                                        c0_ = j * 256 + g * 64
                                        mm(psF[:, bk, c0_:c0_ + 64], wsT[:, g, :], zn[:, j, g * 64:(g + 1) * 64], True, True,
                                           [("wsT",), K("zn")], [("pf", bk)])
                            L.append(mix_mms)
                            L.append(lambda: tt("dve", mb[:].rearrange("p b (g c) -> p b g c", c=64),
                                                psF[:, mbk, :].rearrange("p (h g c) -> p h g c", h=NB, c=64),
                                                bsT[:, :].unsqueeze(1).unsqueeze(3).broadcast_to([128, NB, 4, 64]), ALU.add,
                                                [("pf", mbk), ("bsT",)], [K("mb")]))
                            L.append(lambda: tt("pool", mb[:], mb[:], guzU[:, i0:i0 + NB, :], ALU.mult, [K("mb")] + ku, [K("mb")]))
                            L.append(lambda: act(sqb[:], mb[:], AF.Square, [K("mb")], [K("sqb")]))
                            L.append(lambda: S.op("dve", lambda e, o=ssg[:, bt, :], a=sqb[:]: e.tensor_reduce(out=o, in_=a, axis=AX.X, op=ALU.add),
                                                  reads=[K("sqb")], writes=[("ssg", bt)]))
                            L.append(lambda: rstd_chain(ssg[:, bt, :], lgg[:, bt, :], rg[:, bt, :], 256.0, ("ssg", bt), ("lgg", bt), ("rg", bt)))
                            L.append(lambda: tt("dve", zc[:], mb[:], rg[:, bt, :].unsqueeze(2).broadcast_to([128, NB, 256]), ALU.mult,
                                                [K("mb"), ("rg", bt)], [K("zc")]))
                            L.append(lambda: tt("pool", gmn[:], zc[:], ggb[:, :].unsqueeze(1).broadcast_to([128, NB, 256]), ALU.mult,
                                                [K("zc"), ("ggb",)], [K("gmn")]))

                            def trs():
                                for j in range(NB):
                                    for c in range(2):
                                        q_ = j * 2 + c
                                        tr(psT[:, tb, toff + q_ * 128: toff + (q_ + 1) * 128], gmn[:, j, c * 128:(c + 1) * 128], ident[:],
                                           [K("gmn"), ("ident",)], [("pf", 6 + tb)])
                            L.append(trs)
                            L.append(lambda: cp("act" if bt % 2 == 0 else "dve",
                                                gmT[:, :, i0 * 128:(i0 + NB) * 128].rearrange("p c (j i) -> p c j i", i=128),
                                                psT[:, tb, toff:toff + NB * 256].rearrange("p (j c i) -> p c j i", c=2, i=128),
                                                [("pf", 6 + tb)], [("gmT", i0 // 4)]))
                            return L

                        for bp in range(0, NBT, NSET):
                            Ls = [gm_batch(bp + k_) for k_ in range(NSET)]
                            for fs in zip(*Ls):
                                for f_ in fs:
                                    f_()
                    S.barrier()
                    with contextlib.ExitStack() as ST:
                        wqkv = [alloc(ST, "wqkv%d" % i, [128, 3, 8, 128], BF16) for i in range(2)]
                        wt = [alloc(ST, "wt%d" % i, [128, 2, 3, 256], BF16) for i in range(2)]
                        kT = alloc(ST, "kT", [128, 4096], BF16)
                        vT = alloc(ST, "vT", [128, 4096], BF16)
                        qT = alloc(ST, "qT", [128, 2048], BF16)
                        vaug = [alloc(ST, "vaug%d" % i, [128, 32, 192], BF16) for i in range(2)]
                        acc = [alloc(ST, "acc%d" % i, [128, 2048], F32) for i in range(2)]
                        rec = alloc(ST, "rec", [128, 1, 512], F32)
                        lnd = alloc(ST, "lnd", [128, 1, 512], F32)
                        P2 = [alloc(ST, "P2_%d" % i, [128, 2, 512], BF16) for i in range(4)]
                        otmp = [alloc(ST, "otmp%d" % i, [128, 512], F32) for i in range(2)]

                        def load_pair_weights(p):
                            wb = p % 2
                            for k, col0 in enumerate((p * 128, 768 + p * 128, 1536 + p * 128)):
                                dma("pool", wqkv[wb][:, k], w_in_v[:, :, col0:col0 + 128], [], [("wqkv", wb, k)])
                            dma("pool", wt[wb][:], wtab_d[:, 2 * p:2 * p + 2], [], [("wt", wb)])

                        for vbi in range(2):
                            if s == 0:
                                cp("pool", vaug[vbi][:, 0:16, 64:128], hm[:, :].unsqueeze(1).broadcast_to([128, 16, 64]),
                                   [("hm",)], [("vaug1", vbi)])
                            else:
                                memset("pool", vaug[vbi][:, 0:16, 64:128], 1.0, [("vaug1", vbi)])
                            memset("pool", vaug[vbi][:, 16:32, 64:128], 1.0, [("vaug1", vbi)])
                        load_pair_weights(0)
                        pat_ctr = 0
                        bank_rr = [0]
                        tb_rr = [0]

                        def proj_bank():
                            bank_rr[0] += 1
                            return bank_rr[0] % 6

                        for p in range(DBG['pairs']):
                            wb = p % 2
                            if p + 1 < 6:
                                load_pair_weights(p + 1)
                            if s == 0 and p < 4:
                                precast_ffn(p)
                            if p == 1:
                                dma("pool", wo[:], w_out_v, [], [("wo",)])
                            ktiles = range(8) if s == 0 else range(4, 8)
                            if s == 1:
                                dma("sp", kT[:, 0:2048], kvs[p, 0], [("kvs", p, 0)], [("kT", t_) for t_ in range(4)])
                                dma("sp", vT[:, 0:2048], kvs[p, 1], [("kvs", p, 1)], [("vT", t_) for t_ in range(4)])
                            for (k, dst, name, tiles) in ((1, kT, "kT", ktiles), (2, vT, "vT", ktiles), (0, qT, "qT", range(4, 8))):
                                for t in tiles:
                                    bank = proj_bank()
                                    hk = [("hnT", 4 * t + j) for j in range(4)]
                                    for c in range(8):
                                        mm(psF[:, bank, :], wqkv[wb][:, k, c, :], hnT[:, c, t * 512:(t + 1) * 512], c == 0, c == 7,
                                           hk + [("wqkv", wb, k)], [("pf", bank)])
                                    if k == 0:
                                        tq = t - 4
                                        if evac_eng() == "act":
                                            act(qT[:, tq * 512:(tq + 1) * 512], psF[:, bank, :], AF.Copy, [("pf", bank)], [("qT", tq)], scale=0.125)
                                        else:
                                            ts1("dve", qT[:, tq * 512:(tq + 1) * 512], psF[:, bank, :], 0.125, ALU.mult, [("pf", bank)], [("qT", tq)])
                                    else:
                                        cp(evac_eng(), dst[:, t * 512:(t + 1) * 512], psF[:, bank, :], [("pf", bank)], [(name, t)])
                                if s == 0 and k in (1, 2):
                                    dma("sp", kvs[p, k - 1], dst[:, 2048:4096], [(name, t_) for t_ in range(4, 8)], [("kvs", p, k - 1)])
                            for pi, d in enumerate(PATTERNS):
                                nbd = 32 // d
                                hb = nbd // 2
                                vbi = pat_ctr % 2
                                pat_ctr += 1
                                vb = vaug[vbi]
                                vv = vT[:].rearrange("p (b i d) -> p d b i", d=d, i=128)
                                kv = kT[:].rearrange("p (b i d) -> p d b i", d=d, i=128)
                                qv = qT[:].rearrange("p (b i d) -> p d b i", d=d, i=128)

                                def tiles_of(r, b, d=d):
                                    lo = (b * 128 * d) // 512
                                    hi = (b * 128 * d + 127 * d + r) // 512
                                    return range(lo, hi + 1)

                                if d == 1:
                                    halo_groups = [[15]]
                                elif d == 4:
                                    halo_groups = [[3, 7, 11, 15]]
                                else:
                                    halo_groups = [list(range(0, 8)), list(range(8, 16))]
                                own_groups = [list(range(16, 24)), list(range(24, 32))]
                                vb4 = vb[:].rearrange("p n (a c) -> p n a c", c=64)
                                for grp in halo_groups + own_groups:
                                    tb = tb_rr[0] % 2
                                    tb_rr[0] += 1
                                    for j, idv in enumerate(grp):
                                        r, b = vid_inv(d, idv)
                                        tr(psT[:, tb, j * 128:(j + 1) * 128], vv[:, r, b, :], ident[:],
                                           [("vT", t) for t in tiles_of(r, b)] + [("ident",)], [("pf", 6 + tb)])
                                    n = len(grp)
                                    step = (grp[1] - grp[0]) if n > 1 else 1
                                    o_ap = vb4[:, grp[0]:grp[0] + step * (n - 1) + 1:step, 0:3:2, :]
                                    i_ap = psT[:, tb, 0:n * 128].rearrange("p (n a c) -> p n a c", a=2, c=64)
                                    wk = [("vaug", vbi, idv) for idv in grp]
                                    if s == 0 and grp[0] < 16:
                                        act(o_ap, i_ap, AF.Copy, [("pf", 6 + tb), ("hm",)], wk, scale=hm[:, 0:1])
                                    else:
                                        cp(evac_eng(), o_ap, i_ap, [("pf", 6 + tb)], wk)

                                if d == 1:
                                    groups = [[(0, 16 + 4 * g + j) for j in range(4)] for g in range(4)]
                                elif d == 4:
                                    groups = [[(r, 4 + j) for j in range(4)] for r in range(4)]
                                else:
                                    groups = [[(4 * g + j, 1) for j in range(4)] for g in range(4)]
                                units = [(gi, half) for gi in range(4) for half in range(2)]
                                ustate = {}
                                pu_rr = [0]
                                pt_rr = [0]
                                og_rr = [0]

                                def emit_qk(u, d=d, hb=hb, kv=kv, qv=qv, groups=groups, pi=pi, wb=wb):
                                    gi, half = units[u]
                                    sA = 2 * (pu_rr[0] % 3)
                                    pu_rr[0] += 1
                                    pt = pt_rr[0] % 4
                                    pt_rr[0] += 1
                                    for jj in range(2):
                                        r, b = groups[gi][2 * half + jj]
                                        qk = [("qT", t - 4) for t in tiles_of(r, b)]
                                        for which, kb_ in enumerate((b - 1, b)):
                                            kk = [("kT", t) for t in tiles_of(r, kb_)]
                                            col = jj * 256 + which * 128
                                            for hh in range(2):
                                                psl = slice(0, 64) if hh == 0 else slice(64, 128)
                                                mm(psF[:, sA + hh, col:col + 128], kv[psl, r, kb_, :], qv[psl, r, b - hb, :], True, True,
                                                   kk + qk, [("pf", sA + hh)])
                                    act(P2[pt][:], psF[:, sA:sA + 2, :], AF.Exp, [("pf", sA), ("pf", sA + 1)], [("P2", pt)])
                                    pv4 = P2[pt][:].rearrange("p h (j k) -> p h j k", k=256)
                                    tt("dve", pv4, pv4, wt[wb][:, :, pi, :].unsqueeze(2).broadcast_to([128, 2, 2, 256]), ALU.mult,
                                       [("P2", pt), ("wt", wb)], [("P2", pt)])
                                    ustate[u] = pt

                                def emit_pv(u, d=d, groups=groups, vb=vb, vbi=vbi, pi=pi):
                                    gi, half = units[u]
                                    pt = ustate.pop(u)
                                    ob = 6
                                    for jj in range(2):
                                        r, b = groups[gi][2 * half + jj]
                                        q = 2 * half + jj
                                        for which, kb_ in enumerate((b - 1, b)):
                                            idk = vid(d, r, kb_)
                                            for hh in range(2):
                                                vcols = slice(0, 128) if hh == 0 else slice(64, 192)
                                                mm(psF[:, ob + hh, q * 128:(q + 1) * 128], vb[:, idk, vcols],
                                                   P2[pt][:, hh, jj * 256 + which * 128: jj * 256 + (which + 1) * 128],
                                                   which == 0, which == 1,
                                                   [("vaug", vbi, idk), ("vaug1", vbi), ("P2", pt)], [("pf", ob + hh)])
                                    if half == 1:
                                        for hh in range(2):
                                            accv = acc[hh][:].rearrange("p (b i d) -> p d b i", d=d, i=128)
                                            if d == 1:
                                                dst = accv[:, 0, 4 * gi:4 * gi + 4, :]
                                            elif d == 4:
                                                dst = accv[:, gi, 0:4, :]
                                            else:
                                                dst = accv[:, 4 * gi:4 * gi + 4, 0, :]
                                            src = psF[:, ob + hh, :].rearrange("p (q i) -> p q i", i=128)
                                            if pi == 0:
                                                cp("act", dst, src, [("pf", ob + hh)], [("acc", hh)])
                                            else:
                                                oi_ = og_rr[0] % 2
                                                og_rr[0] += 1
                                                cp("act", otmp[oi_][:].rearrange("p (q i) -> p q i", i=128), src, [("pf", ob + hh)], [("otmp", oi_)])
                                                tt("pool", dst, dst, otmp[oi_][:].rearrange("p (q i) -> p q i", i=128), ALU.add,
                                                   [("otmp", oi_), ("acc", hh)], [("acc", hh)])

                                SKEW = 2
                                nu = len(units)
                                for u in range(min(SKEW, nu)):
                                    emit_qk(u)
                                for u in range(nu):
                                    if u + SKEW < nu:
                                        emit_qk(u + SKEW)
                                    emit_pv(u)
                            for hh in range(2):
                                nps = slice(0, 64) if hh == 0 else slice(64, 128)
                                dps = slice(64, 128) if hh == 0 else slice(0, 64)
                                for ch in range(4):
                                    cols = slice(ch * 512, (ch + 1) * 512)
                                    rb = 0
                                    act(lnd[dps, rb, :], acc[hh][dps, cols], AF.Ln, [("acc", hh)], [("lnd", hh, rb)])
                                    act(rec[nps, rb, :], lnd[dps, rb, :], AF.Exp, [("lnd", hh, rb)], [("rec", hh, rb)], scale=-1.0)
                                    stt("dve", attnT[nps, p, cols], acc[hh][nps, cols], gacol[nps, p:p + 1], rec[nps, rb, :],
                                        ALU.mult, ALU.mult, [("acc", hh), ("gacol",), ("rec", hh, rb)], [("attnT", p, hh, ch)])
                    S.barrier()
                S.barrier()
                with contextlib.ExitStack() as SC:
                    NT = DBG['tilesC'] if DBG['phaseC'] else 0
                    g2bc = alloc(SC, "g2bc", [128, 1024], F32)
                    gfbc = alloc(SC, "gfbc", [128, 1024], F32)
                    ones_bb = alloc(SC, "ones_bb", [128, 128], BF16)
                    sq = alloc(SC, "sq", [128, 6, 512], BF16)
                    attnN = alloc(SC, "attnN", [128, 6, 512], BF16)
                    lnra = alloc(SC, "lnra", [128, 512], F32)
                    rabc = alloc(SC, "rabc", [128, 512], F32)
                    h1 = [alloc(SC, "h1_%d" % i, [128, 4, 1024], F32) for i in range(2)]
                    hn2 = [alloc(SC, "hn2_%d" % i, [128, 1024], BF16) for i in range(2)]
                    hn2T = [alloc(SC, "hn2T%d" % i, [128, 8, 512], BF16) for i in range(2)]
                    aT = alloc(SC, "aT", [128, 32, 512], BF16)
                    rl = [alloc(SC, "rl%d" % i, [128, 512], F32) for i in range(3)]
                    W1b = [alloc(SC, "W1b%d" % i, [128, 8, 512], BF16) for i in range(2)]
                    W2b = [alloc(SC, "W2b%d" % i, [128, 4, 512], BF16) for i in range(3)]
                    ot = [alloc(SC, "ot%d" % i, [128, 1024], F32) for i in range(2)]
                    junkc = alloc(SC, "junkc", [128, 1024], BF16)
                    ss2 = alloc(SC, "ss2", [128, 16], F32)
                    ln2 = alloc(SC, "ln2", [128, 16], F32)
                    r2 = alloc(SC, "r2", [128, 16], F32)
                    ssf = alloc(SC, "ssf", [128, 16, 2], F32)
                    ssft = alloc(SC, "ssft", [128, 16], F32)
                    lnf = alloc(SC, "lnf", [128, 16], F32)
                    rf = alloc(SC, "rf", [128, 16], F32)
                    dma("sp", g2bc[:], g2_d, [], [("g2bc",)])
                    dma("sp", gfbc[:], gf_d, [], [("gfbc",)])
                    memset("pool", ones_bb[:], 1.0, [("ones_bb",)])
                    cst = dict(w1=0, w2=0, rl=0, ot=0)
                    w1_slot = {}
                    w2_slot = {}
                    units2 = [(h, g) for h in range(2) for g in range(8)]

                    def issue_w1(t, g):
                        sl = cst["w1"] % 2
                        cst["w1"] += 1
                        dma("sp", W1b[sl][:], w1s[:, g], [("w1s", g)], [("W1b", sl)])
                        w1_slot[(t, g)] = sl

                    def issue_w2(t, u):
                        h, g = units2[u]
                        sl = cst["w2"] % 3
                        cst["w2"] += 1
                        dma("sp", W2b[sl][:], w2s[:, h, g], [("w2s", h, g)], [("W2b", sl)])
                        w2_slot[(t, u)] = sl

                    def akeys(t):
                        return [("attnT", p_, h_, t) for p_ in range(6) for h_ in range(2)]

                    wprep_done = set()

                    def Wprep_c(t, bank):
                        if t in wprep_done:
                            return
                        wprep_done.add(t)
                        T0 = t * 512
                        tt("pool", sq[:], attnT[:, :, T0:T0 + 512], attnT[:, :, T0:T0 + 512], ALU.mult, akeys(t), [("sq",)])
                        for c in range(6):
                            mm(psF[:, bank, :], ones_bb[:], sq[:, c, :], c == 0, c == 5, [("sq",), ("ones_bb",)], [("pf", bank)])
                        act(lnra[:], psF[:, bank, :], AF.Ln, [("pf", bank)], [("lnra",)], scale=1.0 / 768, bias=EPS)
                        act(rabc[:], lnra[:], AF.Exp, [("lnra",)], [("rabc",)], scale=-0.5)
                        tt("pool", attnN[:], attnT[:, :, T0:T0 + 512], rabc[:, :].unsqueeze(1).broadcast_to([128, 6, 512]), ALU.mult,
                           akeys(t) + [("rabc",)], [("attnN",)])

                    def Wprep(t):
                        T0 = t * 512
                        Wprep_c(t, 0)
                        for blk in range(4):
                            row0 = s * 2048 + T0 + blk * 128
                            dma("sp", h1[t % 2][:, blk, :], x_own[row0:row0 + 128, :], [], [("h1", t % 2, blk)])

                    def WP1(t, blk):
                        c0 = t * 512 + blk * 128
                        bb = 2 * (blk % 2)
                        for half in range(2):
                            for c in range(8):
                                lhs = attnN[:, c, blk * 128:(blk + 1) * 128] if c < 6 else gmT[:, c - 6, c0:c0 + 128]
                                mm(psF[:, bb + half, :], lhs, wo[:, c, half * 512:(half + 1) * 512], c == 0, c == 7,
                                   [("attnN",), ("gmT", t), ("wo",)], [("pf", bb + half)])
                        hv = h1[t % 2][:, blk, :].rearrange("p (h n) -> p h n", n=512)
                        tt("dve", hv, hv, psF[:, bb:bb + 2, :], ALU.add,
                           [("pf", bb), ("pf", bb + 1), ("h1", t % 2, blk)], [("h1", t % 2, blk)])

                    def WA(t, blk):
                        bi = t * 4 + blk
                        act(junkc[:], h1[t % 2][:, blk, :], AF.Square, [("h1", t % 2, blk)], [("junkc",), ("ss2", bi)],
                            accum_out=ss2[:, bi:bi + 1])
                        rstd_chain(ss2[:, bi:bi + 1], ln2[:, bi:bi + 1], r2[:, bi:bi + 1], 1024.0, ("ss2", bi), ("ln2", bi), ("r2", bi))
                        hb_ = blk % 2
                        stt("dve", hn2[hb_][:], h1[t % 2][:, blk, :], r2[:, bi:bi + 1], g2bc[:], ALU.mult, ALU.mult,
                            [("h1", t % 2, blk), ("r2", bi), ("g2bc",)], [("hn2", hb_)])

                    def WP2(t, blk):
                        hb_ = blk % 2
                        tb = blk % 2
                        for c in range(8):
                            tr(psT[:, tb, c * 128:(c + 1) * 128], hn2[hb_][:, c * 128:(c + 1) * 128], ident[:],
                               [("hn2", hb_), ("ident",)], pTkeys(tb))
                        cp("dve", hn2T[t % 2][:, :, blk * 128:(blk + 1) * 128],
                           psT[:, tb, :].rearrange("p (c i) -> p c i", i=128), pTkeys(tb), [("hn2T", t % 2, blk)])

                    def ff1_group(t, g):
                        hkeys = [("hn2T", t % 2, b_) for b_ in range(4)]
                        cur = w1_slot[(t, g)]
                        for cc in range(4):
                            ffc = 4 * g + cc
                            bank = 4 + ffc % 2
                            for c in range(8):
                                mm(psF[:, bank, :], W1b[cur][:, c, cc * 128:(cc + 1) * 128], hn2T[t % 2][:, c, :], c == 0, c == 7,
                                   hkeys + [("W1b", cur)], [("pf", bank)])
                            ri = cst["rl"] % 3
                            cst["rl"] += 1
                            act(rl[ri][:], psF[:, bank, :], AF.Relu, [("pf", bank)], [("rl", ri)])
                            tt("pool", aT[:, ffc, :], rl[ri][:], rl[ri][:], ALU.mult, [("rl", ri)], [("aT", ffc)])

                    def ff2(t):
                        for u, (h, g) in enumerate(units2):
                            if u + 2 < len(units2):
                                issue_w2(t, u + 2)
                            if u == 12 and t + 1 < NT:
                                issue_w1(t + 1, 0)
                            if u == 3 and t + 2 < NT:
                                Wprep_c(t + 2, 4)
                            sl = w2_slot[(t, u)]
                            for blk in range(4):
                                for cc in range(4):
                                    ffc = 4 * g + cc
                                    mm(psF[:, blk, :], aT[:, ffc, blk * 128:(blk + 1) * 128], W2b[sl][:, cc, :],
                                       g == 0 and cc == 0, g == 7 and cc == 3,
                                       [("aT", ffc), ("W2b", sl)], [("pf", blk)])
                            if g == 7:
                                for blk in range(4):
                                    bi = t * 4 + blk
                                    hsl = h1[t % 2][:, blk, h * 512:(h + 1) * 512]
                                    tt("dve", hsl, hsl, psF[:, blk, :], ALU.add, [("pf", blk), ("h1", t % 2, blk)], [("h1", t % 2, blk)])
                                    act(junkc[:, 0:512], hsl, AF.Square, [("h1", t % 2, blk)],
                                        [("junkc",), ("ssf", bi, h)], accum_out=ssf[:, bi, h:h + 1])

                    def final(t):
                        for blk in range(4):
                            bi = t * 4 + blk
                            tt("dve", ssft[:, bi:bi + 1], ssf[:, bi, 0:1], ssf[:, bi, 1:2], ALU.add,
                               [("ssf", bi, 0), ("ssf", bi, 1)], [("ssft", bi)])
                            rstd_chain(ssft[:, bi:bi + 1], lnf[:, bi:bi + 1], rf[:, bi:bi + 1], 1024.0, ("ssft", bi), ("lnf", bi), ("rf", bi))
                            oi = cst["ot"] % 2
                            cst["ot"] += 1
                            stt("dve", ot[oi][:], h1[t % 2][:, blk, :], rf[:, bi:bi + 1], gfbc[:], ALU.mult, ALU.mult,
                                [("h1", t % 2, blk), ("rf", bi), ("gfbc",)], [("ot", oi)])
                            row0 = s * 2048 + t * 512 + blk * 128
                            out_ops.append(dma("sp", out_d[row0:row0 + 128, :], ot[oi][:], [("ot", oi)], [("out", row0)]))

                    if NT > 0:
                        issue_w1(0, 0)
                        Wprep(0)
                        WP1(0, 0)
                        WP1(0, 1)
                        WA(0, 0)
                        WA(0, 1)
                        WP2(0, 0)
                        WP1(0, 2)
                        WP2(0, 1)
                        WP1(0, 3)
                        WA(0, 2)
                        WA(0, 3)
                        WP2(0, 2)
                        WP2(0, 3)
                    for t in range(NT):
                        nxt = t + 1 < NT
                        for g in range(8):
                            if g + 1 < 8:
                                issue_w1(t, g + 1)
                            if g == 6:
                                issue_w2(t, 0)
                                issue_w2(t, 1)
                            if nxt:
                                if g == 0:
                                    Wprep(t + 1)
                                if g < 4:
                                    WP1(t + 1, g)
                                if 1 <= g < 5:
                                    WA(t + 1, g - 1)
                                if 2 <= g < 6:
                                    WP2(t + 1, g - 2)
                            ff1_group(t, g)
                        ff2(t)
                        final(t)
                S.barrier()
        S.emit(final_wait_ops=out_ops)
    return nc, S


_CACHE = {}


def _host_consts():
    if "c" not in _CACHE:
        _CACHE["c"] = dict(
            wtab=make_wtab(),
            tri=np.triu(np.ones((128, 128), dtype=np.float32)),
            ident=np.eye(128, dtype=np.float32),
        )
    return _CACHE["c"]


def kernel(x, norm1_g, w_in, sgu_ln_g, sgu_ln_b, sgu_w, sgu_b, attn_out_g, gmlp_out_g,
           w_out, norm2_g, w_ff1, w_ff2, final_norm_g):
    f = lambda a: np.ascontiguousarray(np.asarray(a, dtype=np.float32))
    x = f(x)
    consts = _host_consts()
    bc = lambda v, n: np.ascontiguousarray(np.broadcast_to(f(v).reshape(1, n), (128, n)))
    shared = dict(
        w_in=f(w_in)[0], w_out=f(w_out)[0], w_ff1=f(w_ff1)[0], w_ff2=f(w_ff2)[0],
        g1_bc=bc(np.asarray(norm1_g)[0], 1024), g2_bc=bc(np.asarray(norm2_g)[0], 1024), gf_bc=bc(final_norm_g, 1024),
        ga_col=np.ascontiguousarray(f(attn_out_g)[0].reshape(6, 128).T),
        gg_bc=bc(np.asarray(gmlp_out_g)[0], 256),
        lng_bc=bc(np.asarray(sgu_ln_g)[0].reshape(-1), 256), lnb_bc=bc(np.asarray(sgu_ln_b)[0].reshape(-1), 256),
        bsT=np.ascontiguousarray(f(sgu_b)[0].T),
        wsT=np.ascontiguousarray(f(sgu_w)[0].transpose(2, 0, 1)),
        tri=consts["tri"], wtab=consts["wtab"], ident=consts["ident"],
    )
    in_maps = []
    for c in range(N_CORES):
        b, half = divmod(c, 2)
        m = dict(shared)
        m["x_own"] = np.ascontiguousarray(x[b, half * 4096:(half + 1) * 4096])
        if half == 1:
            m["x_halo"] = np.ascontiguousarray(x[b, 2048:4096])
            m["hmask"] = np.ones((128, 64), dtype=np.float32)
        else:
            m["x_halo"] = np.zeros((2048, 1024), dtype=np.float32)
            m["hmask"] = np.zeros((128, 64), dtype=np.float32)
        in_maps.append(m)
    if "nc" not in _CACHE:
        _CACHE["nc"] = build_program()
    nc, _ = _CACHE["nc"]
    res = run_bass_kernel_spmd(nc, in_maps, core_ids=list(range(N_CORES)))
    outp = np.empty((4, 8192, 1024), dtype=np.float32)
    for c in range(N_CORES):
        b, half = divmod(c, 2)
        outp[b, half * 4096:(half + 1) * 4096] = res.results[c]["out"]
    return outp
```

```python
import math
import contextlib
import numpy as np
import concourse.bass as bass
import concourse.mybir as mybir
from concourse.bass_utils import run_bass_kernel_spmd

F32 = mybir.dt.float32
BF16 = mybir.dt.bfloat16
AF = mybir.ActivationFunctionType
ALU = mybir.AluOpType
AX = mybir.AxisListType
EPS = 1e-6
PATTERNS = (1, 4, 16)
N_CORES = 8
DBG = dict(spans=2, pairs=6, phaseC=True, gmlp=True, tilesC=4)


class Sched:
    ENGS = ("pe", "act", "dve", "pool", "sp")

    def __init__(self, nc):
        self.nc = nc
        self.eng = {"pe": nc.tensor, "act": nc.scalar, "dve": nc.vector, "pool": nc.gpsimd, "sp": nc.sync}
        self.ops = []
        self.last_w = {}
        self.readers = {}
        self.last_by_eng = {}
        self.dma_since_barrier = []
        self.pending_barrier = {}

    def barrier(self):
        deps = set(self.last_by_eng.values()) | set(self.dma_since_barrier)
        self.dma_since_barrier = []
        for e in self.ENGS:
            self.pending_barrier.setdefault(e, set()).update(deps)

    def op(self, eng, fn, reads=(), writes=(), dma=False):
        idx = len(self.ops)
        deps = set()
        raw = set()
        for k in reads:
            w = self.last_w.get(k)
            if w is not None:
                deps.add(w)
                raw.add(w)
        for k in writes:
            w = self.last_w.get(k)
            if w is not None:
                deps.add(w)
            for r in self.readers.get(k, {}).values():
                deps.add(r)
        pb = self.pending_barrier.pop(eng, None)
        if pb:
            deps |= pb
            raw |= pb
        deps.discard(idx)
        self.ops.append(dict(eng=eng, fn=fn, deps=deps, raw=raw, dma=dma))
        for k in reads:
            rd = self.readers.setdefault(k, {})
            if dma:
                rd[("dma", idx)] = idx
            else:
                rd[eng] = idx
        for k in writes:
            self.last_w[k] = idx
            self.readers[k] = {}
        if dma:
            self.dma_since_barrier.append(idx)
        else:
            self.last_by_eng[eng] = idx
        return idx

    def _needs_sem(self, o, d, od):
        if od["dma"]:
            return True
        if od["eng"] != o["eng"]:
            return True
        if o["dma"]:
            return True
        return o["eng"] != "pe"

    def emit(self, final_wait_ops=()):
        nc = self.nc
        ops = self.ops
        n = len(ops)
        need = [False] * n
        for o in ops:
            for d in o["deps"]:
                if self._needs_sem(o, d, ops[d]):
                    need[d] = True
        for d in final_wait_ops:
            need[d] = True
        with contextlib.ExitStack() as st:
            esem = {e: st.enter_context(nc.semaphore("s_" + e)) for e in self.ENGS}
            NDMA = 40
            dsem = [st.enter_context(nc.semaphore("d%d" % i)) for i in range(NDMA)]
            slots_of = {"sp": list(range(0, 24)), "pool": list(range(24, 40))}
            slot_rr = {"sp": 0, "pool": 0}
            ecount = {e: 0 for e in self.ENGS}
            dcount = [0] * NDMA
            dnext = 0
            sig = [None] * n
            waited = {e: {} for e in self.ENGS}
            nwaits = 0

            def do_wait(e, s):
                nonlocal nwaits
                sem, val, key = s
                if waited[e].get(key, -1) >= val:
                    return
                self.eng[e].wait_ge(sem, val)
                waited[e][key] = val
                nwaits += 1

            for i, o in enumerate(ops):
                e = o["eng"]
                for d in sorted(o["deps"]):
                    if need[d] and self._needs_sem(o, d, ops[d]):
                        do_wait(e, sig[d])
                if o["dma"]:
                    slot = slots_of[e][slot_rr[e] % len(slots_of[e])]
                    slot_rr[e] += 1
                    dnext += 1
                    if dcount[slot] > 0:
                        do_wait(e, (dsem[slot], dcount[slot], ("d", slot)))
                    inst = o["fn"](self.eng[e])
                    dcount[slot] += 16
                    inst.then_inc(dsem[slot], 16)
                    sig[i] = (dsem[slot], dcount[slot], ("d", slot))
                else:
                    inst = o["fn"](self.eng[e])
                    if need[i]:
                        ecount[e] += 1
                        inst.then_inc(esem[e], 1)
                        sig[i] = (esem[e], ecount[e], ("e", e))
            for d in final_wait_ops:
                do_wait("sp", sig[d])
            self.stats = dict(n_ops=n, ecount=dict(ecount), ndma=dnext, nwaits=nwaits)


def alibi_slopes(n):
    def pow2_slopes(m):
        start = 2.0 ** (-8.0 / m)
        return [start ** (i + 1) for i in range(m)]
    if math.log2(n).is_integer():
        s = pow2_slopes(n)
    else:
        c = 2 ** int(math.floor(math.log2(n)))
        s = pow2_slopes(c) + pow2_slopes(2 * c)[0::2][: n - c]
    return np.asarray(s, dtype=np.float32)


def make_wtab():
    sl = alibi_slopes(12).astype(np.float64)
    i = np.arange(128)[:, None]
    j = np.arange(128)[None, :]
    tab = np.zeros((128, 12, 3, 256), dtype=np.float64)
    for h in range(12):
        for pi, d in enumerate(PATTERNS):
            steps_prev = j + 128 - i
            steps_cur = j - i
            tab[:, h, pi, 0:128] = np.where(j <= i, np.exp(-sl[h] * d * np.maximum(steps_prev, 0)), 0.0)
            tab[:, h, pi, 128:256] = np.where(j >= i, np.exp(-sl[h] * d * np.maximum(steps_cur, 0)), 0.0)
    return tab.astype(np.float32)


def vid(d, r, b):
    hb = (32 // d) // 2
    return (b // hb) * 16 + r * hb + (b % hb)


def vid_inv(d, i):
    hb = (32 // d) // 2
    sec, w = divmod(i, 16)
    r, bb = divmod(w, hb)
    return r, sec * hb + bb


def build_program():
    nc = bass.Bass("TRN2", target_bir_lowering=False)

    def din(name, shape):
        return nc.dram_tensor(name, shape, F32, kind="ExternalInput").ap()

    x_own = din("x_own", [4096, 1024])
    x_halo = din("x_halo", [2048, 1024])
    hmask_d = din("hmask", [128, 64])
    w_in = din("w_in", [1024, 2816])
    w_out = din("w_out", [1024, 1024])
    w_ff1 = din("w_ff1", [1024, 4096])
    w_ff2 = din("w_ff2", [4096, 1024])
    g1_d = din("g1_bc", [128, 1024])
    g2_d = din("g2_bc", [128, 1024])
    gf_d = din("gf_bc", [128, 1024])
    ga_d = din("ga_col", [128, 6])
    gg_d = din("gg_bc", [128, 256])
    lng_d = din("lng_bc", [128, 256])
    lnb_d = din("lnb_bc", [128, 256])
    bs_d = din("bsT", [128, 4])
    ws_d = din("wsT", [128, 4, 128])
    tri_d = din("tri", [128, 128])
    wtab_d = din("wtab", [128, 12, 3, 256])
    ident_d = din("ident", [128, 128])
    out_d = nc.dram_tensor("out", [4096, 1024], F32, kind="ExternalOutput").ap()
    w1s = nc.dram_tensor("w1s", [128, 8, 8, 512], BF16, kind="Internal").ap()
    w2s = nc.dram_tensor("w2s", [128, 2, 8, 4, 512], BF16, kind="Internal").ap()
    kvs = nc.dram_tensor("kvs", [6, 2, 128, 2048], BF16, kind="Internal").ap()

    w_in_v = w_in.rearrange("(c p) n -> p c n", p=128)
    w_out_v = w_out.rearrange("(c p) n -> p c n", p=128)
    w_ff1_v = w_ff1.rearrange("(c p) n -> p c n", p=128)
    w_ff2_v = w_ff2.rearrange("(c p) n -> p c n", p=128)

    S = Sched(nc)
    out_ops = []
    uid = [0]

    def alloc(st, name, shape, dt):
        uid[0] += 1
        return st.enter_context(nc.sbuf_tensor("%s_%d" % (name, uid[0]), shape, dt))

    def dma(q, out, in_, reads, writes):
        return S.op(q, lambda e, out=out, in_=in_: e.dma_start(out=out, in_=in_), reads=reads, writes=writes, dma=True)

    def mm(out, lhsT, rhs, start, stop, reads, writes):
        S.op("pe", lambda e, out=out, lhsT=lhsT, rhs=rhs, start=start, stop=stop:
             e.matmul(out, lhsT=lhsT, rhs=rhs, start=start, stop=stop), reads=reads, writes=writes)

    def tr(out, in_, ident, reads, writes):
        S.op("pe", lambda e, out=out, in_=in_, ident=ident: e.transpose(out, in_, ident), reads=reads, writes=writes)

    def act(out, in_, func, reads, writes, **kw):
        S.op("act", lambda e, out=out, in_=in_, func=func, kw=kw: e.activation(out=out, in_=in_, func=func, **kw),
             reads=reads, writes=writes)

    def tt(eng, out, in0, in1, op, reads, writes):
        S.op(eng, lambda e, out=out, in0=in0, in1=in1, op=op: e.tensor_tensor(out=out, in0=in0, in1=in1, op=op),
             reads=reads, writes=writes)

    def stt(eng, out, in0, scalar, in1, op0, op1, reads, writes):
        S.op(eng, lambda e, out=out, in0=in0, scalar=scalar, in1=in1, op0=op0, op1=op1:
             e.scalar_tensor_tensor(out=out, in0=in0, scalar=scalar, in1=in1, op0=op0, op1=op1), reads=reads, writes=writes)

    def ts1(eng, out, in0, scalar, op, reads, writes):
        S.op(eng, lambda e, out=out, in0=in0, scalar=scalar, op=op:
             e.tensor_scalar(out=out, in0=in0, scalar1=scalar, scalar2=None, op0=op), reads=reads, writes=writes)

    def cp(eng, out, in_, reads, writes):
        if eng == "act":
            S.op("act", lambda e, out=out, in_=in_: e.copy(out=out, in_=in_), reads=reads, writes=writes)
        else:
            S.op(eng, lambda e, out=out, in_=in_: e.tensor_copy(out=out, in_=in_), reads=reads, writes=writes)

    def memset(eng, ap, val, writes):
        S.op(eng, lambda e, ap=ap, val=val: e.memset(ap, val), writes=writes)

    def rstd_chain(ss_ap, ln_ap, rs_ap, n, kss, kln, krs):
        act(ln_ap, ss_ap, AF.Ln, [kss], [kln], scale=1.0 / n, bias=EPS)
        act(rs_ap, ln_ap, AF.Exp, [kln], [krs], scale=-0.5)

    evac_rr = [0]

    def evac_eng():
        evac_rr[0] += 1
        return "act" if evac_rr[0] % 2 else "dve"

    with contextlib.ExitStack() as G:
        psF = G.enter_context(nc.psum_tensor("ps8", [128, 8, 512], F32))
        psT = psF[:, 6:8, :].bitcast(BF16)
        ident = alloc(G, "ident", [128, 128], BF16)
        hm = alloc(G, "hm", [128, 64], F32)
        gacol = alloc(G, "gacol", [128, 6], F32)
        onesb = alloc(G, "onesb", [128, 1], BF16)
        dma("pool", ident[:], ident_d, [], [("ident",)])
        dma("sp", hm[:], hmask_d, [], [("hm",)])
        dma("sp", gacol[:], ga_d, [], [("gacol",)])
        memset("pool", onesb[:], 1.0, [("onesb",)])

        def pTkeys(tb):
            return [("pf", 6 + tb)]

        def precast_ffn(part):
            jobs = [("w1", g) for g in range(8)] + [("w2", h, g) for h in range(2) for g in range(8)]
            for job in jobs[part * 6:(part + 1) * 6]:
                if job[0] == "w1":
                    g = job[1]
                    dma("pool", w1s[:, g], w_ff1_v[:, :, g * 512:(g + 1) * 512], [], [("w1s", g)])
                else:
                    _, h, g = job
                    dma("pool", w2s[:, h, g], w_ff2_v[:, 4 * g:4 * g + 4, h * 512:(h + 1) * 512], [], [("w2s", h, g)])

        for s in range(DBG['spans']):
            with contextlib.ExitStack() as SP:
                attnT = alloc(SP, "attnT", [128, 6, 2048], BF16)
                gmT = alloc(SP, "gmT", [128, 2, 2048], BF16)
                wo = alloc(SP, "wo", [128, 8, 1024], BF16)
                with contextlib.ExitStack() as SAB:
                    hnT = alloc(SAB, "hnT", [128, 8, 4096], BF16)
                    with contextlib.ExitStack() as SA:
                        NXB = 3
                        g1bc = alloc(SA, "g1bc", [128, 1024], F32)
                        xt = [alloc(SA, "xt%d" % i, [128, 2, 1024], F32) for i in range(NXB)]
                        xs = [alloc(SA, "xs%d" % i, [128, 1024], BF16) for i in range(4)]
                        junk = alloc(SA, "junkA", [128, 1024], BF16)
                        ssA = alloc(SA, "ssA", [128, 32], F32)
                        lnA = alloc(SA, "lnA", [128, 32], F32)
                        rsA = alloc(SA, "rsA", [128, 32], F32)
                        dma("sp", g1bc[:], g1_d, [], [("g1bc",)])
                        xctr = [0]
                        xbuf_of = {}

                        def A1(kb):
                            if kb % 2 == 0:
                                if kb < 16:
                                    src = (x_halo if s == 0 else x_own)[kb * 128:(kb + 2) * 128, :]
                                else:
                                    r0 = s * 2048 + (kb - 16) * 128
                                    src = x_own[r0:r0 + 256, :]
                                bq = xctr[0] % NXB
                                xctr[0] += 1
                                xbuf_of[kb] = bq
                                xbuf_of[kb + 1] = bq
                                dma("sp", xt[bq][:], src.rearrange("(j p) f -> p j f", p=128), [], [("xt", bq, 0), ("xt", bq, 1)])
                            bq = xbuf_of[kb]
                            j = kb % 2
                            b = kb % 4
                            act(junk[:], xt[bq][:, j, :], AF.Square, [("xt", bq, j)], [("junkA",), ("ssA", kb)],
                                accum_out=ssA[:, kb:kb + 1])
                            rstd_chain(ssA[:, kb:kb + 1], lnA[:, kb:kb + 1], rsA[:, kb:kb + 1], 1024.0,
                                       ("ssA", kb), ("lnA", kb), ("rsA", kb))
                            stt("dve", xs[b][:], xt[bq][:, j, :], rsA[:, kb:kb + 1], g1bc[:], ALU.mult, ALU.mult,
                                [("xt", bq, j), ("rsA", kb), ("g1bc",)], [("xs", b)])

                        def A2(kb):
                            b = kb % 4
                            tb = kb % 2
                            for c in range(8):
                                tr(psT[:, tb, c * 128:(c + 1) * 128], xs[b][:, c * 128:(c + 1) * 128], ident[:],
                                   [("xs", b), ("ident",)], pTkeys(tb))
                            cp(evac_eng(), hnT[:, :, kb * 128:(kb + 1) * 128],
                               psT[:, tb, :].rearrange("p (c i) -> p c i", i=128), pTkeys(tb), [("hnT", kb)])

                        SKA = 2
                        kbs = list(range(32)) if s == 0 else list(range(16, 32))
                        for ii in range(len(kbs) + SKA):
                            if ii < len(kbs):
                                A1(kbs[ii])
                            if ii - SKA >= 0:
                                A2(kbs[ii - SKA])
                    S.barrier()
                    with contextlib.ExitStack() as SG:
                        wuz = alloc(SG, "wuz", [128, 8, 512], BF16)
                        wsT = alloc(SG, "wsT", [128, 4, 128], BF16)
                        wsf = alloc(SG, "wsf", [128, 4, 128], F32)
                        tri = alloc(SG, "tri", [128, 128], F32)
                        lng = alloc(SG, "lng", [128, 256], F32)
                        lnb = alloc(SG, "lnb", [128, 256], F32)
                        ggb = alloc(SG, "ggb", [128, 256], F32)
                        bsT = alloc(SG, "bsT", [128, 4], F32)
                        guzU = alloc(SG, "guzU", [128, 16, 256], F32)
                        guzZ = alloc(SG, "guzZ", [128, 16, 256], F32)
                        NB = 2
                        NSET = 4
                        NBT = (16 // NB) if DBG['gmlp'] else 0
                        sqb_ = [alloc(SG, "sqb%d" % i, [128, NB, 256], F32) for i in range(NSET)]
                        zc_ = [alloc(SG, "zc%d" % i, [128, NB, 256], F32) for i in range(NSET)]
                        zc2_ = [alloc(SG, "zc2%d" % i, [128, NB, 256], F32) for i in range(NSET)]
                        mb_ = [alloc(SG, "mb%d" % i, [128, NB, 256], F32) for i in range(NSET)]
                        zn_ = [alloc(SG, "zn%d" % i, [128, NB, 256], BF16) for i in range(NSET)]
                        gmn_ = [alloc(SG, "gmn%d" % i, [128, NB, 256], BF16) for i in range(NSET)]
                        msum = alloc(SG, "msum", [128, 8, 8], F32)
                        qsum = alloc(SG, "qsum", [128, 8, 8], F32)
                        mean = alloc(SG, "mean", [128, 8, 8], F32)
                        m2 = alloc(SG, "m2", [128, 8, 8], F32)
                        var = alloc(SG, "var", [128, 8, 8], F32)
                        lnv = alloc(SG, "lnv", [128, 8, 8], F32)
                        rsl = alloc(SG, "rsl", [128, 8, 8], F32)
                        ssg = alloc(SG, "ssg", [128, 8, 2], F32)
                        lgg = alloc(SG, "lgg", [128, 8, 2], F32)
                        rg = alloc(SG, "rg", [128, 8, 2], F32)
                        dma("pool", wuz[:], w_in_v[:, :, 2304:2816], [], [("wuz",)])
                        dma("sp", wsf[:], ws_d, [], [("wsf",)])
                        dma("sp", tri[:], tri_d, [], [("tri",)])
                        dma("sp", lng[:], lng_d, [], [("lng",)])
                        dma("sp", lnb[:], lnb_d, [], [("lnb",)])
                        dma("sp", ggb[:], gg_d, [], [("ggb",)])
                        dma("sp", bsT[:], bs_d, [], [("bsT",)])
                        tt("dve", wsT[:], wsf[:], tri[:].unsqueeze(1).broadcast_to([128, 4, 128]), ALU.mult,
                           [("wsf",), ("tri",)], [("wsT",)])
                        for i in range(16 if DBG['gmlp'] else 0):
                            kb = 16 + i
                            bank = 4 + i % 2
                            for c in range(8):
                                mm(psF[:, bank, :], hnT[:, c, kb * 128:(kb + 1) * 128], wuz[:, c, :], c == 0, c == 7,
                                   [("hnT", kb), ("wuz",)], [("pf", bank)])
                            act(guzU[:, i, :], psF[:, bank, 0:256], AF.Gelu_apprx_tanh, [("pf", bank)], [("guzU", i // 8)])
                            act(guzZ[:, i, :], psF[:, bank, 256:512], AF.Gelu_apprx_tanh, [("pf", bank)], [("guzZ", i // 8)])
                        def gm_batch(bt):
                            L = []
                            i0 = bt * NB
                            st_ = bt % NSET
                            sqb, zc, zc2, mb, zn, gmn = sqb_[st_], zc_[st_], zc2_[st_], mb_[st_], zn_[st_], gmn_[st_]
                            K = lambda n: (n, st_)
                            zfl = guzZ[:, i0:i0 + NB, :]
                            zv = zfl.rearrange("p b (g c) -> p (b g) c", c=64)
                            kz = [("guzZ", i0 // 8)]
                            ku = [("guzU", i0 // 8)]
                            zc3 = zc[:].rearrange("p b (g c) -> p (b g) c", c=64)
                            zc23 = zc2[:].rearrange("p b (g c) -> p (b g) c", c=64)
                            mbk = st_
                            tb = bt % 2
                            toff = ((bt // 2) % 2) * 512
                            L.append(lambda: S.op("dve", lambda e, o=msum[:, bt, :], a=zv: e.tensor_reduce(out=o, in_=a, axis=AX.X, op=ALU.add),
                                                  reads=kz, writes=[("msum", bt)]))
                            L.append(lambda: act(sqb[:], zfl, AF.Square, kz, [K("sqb")]))
                            L.append(lambda: S.op("dve", lambda e, o=qsum[:, bt, :], a=sqb[:].rearrange("p b (g c) -> p (b g) c", c=64):
                                                  e.tensor_reduce(out=o, in_=a, axis=AX.X, op=ALU.add), reads=[K("sqb")], writes=[("qsum", bt)]))
                            L.append(lambda: ts1("dve", mean[:, bt, :], msum[:, bt, :], 1.0 / 64, ALU.mult, [("msum", bt)], [("mean", bt)]))
                            L.append(lambda: tt("dve", m2[:, bt, :], mean[:, bt, :], mean[:, bt, :], ALU.mult, [("mean", bt)], [("m2", bt)]))
                            L.append(lambda: stt("dve", var[:, bt, :], qsum[:, bt, :], 1.0 / 64, m2[:, bt, :], ALU.mult, ALU.subtract,
                                                 [("qsum", bt), ("m2", bt)], [("var", bt)]))
                            L.append(lambda: act(lnv[:, bt, :], var[:, bt, :], AF.Ln, [("var", bt)], [("lnv", bt)], bias=EPS))
                            L.append(lambda: act(rsl[:, bt, :], lnv[:, bt, :], AF.Exp, [("lnv", bt)], [("rsl", bt)], scale=-0.5))
                            L.append(lambda: tt("dve", zc3, zv, mean[:, bt, :].unsqueeze(2).broadcast_to([128, 4 * NB, 64]), ALU.subtract,
                                                kz + [("mean", bt)], [K("zc")]))
                            L.append(lambda: tt("dve", zc23, zc3, rsl[:, bt, :].unsqueeze(2).broadcast_to([128, 4 * NB, 64]), ALU.mult,
                                                [K("zc"), ("rsl", bt)], [K("zc2")]))
                            L.append(lambda: tt("dve", zc[:], zc2[:], lng[:, :].unsqueeze(1).broadcast_to([128, NB, 256]), ALU.mult,
                                                [K("zc2"), ("lng",)], [K("zc")]))
                            L.append(lambda: tt("dve", zn[:], zc[:], lnb[:, :].unsqueeze(1).broadcast_to([128, NB, 256]), ALU.add,
                                                [K("zc"), ("lnb",)], [K("zn")]))

                            def mix_mms():
                                for j in range(NB):
                                    bk = mbk
                                    for g in range(4):
                                        c0_ = j * 256 + g * 64
                                        mm(psF[:, bk, c0_:c0_ + 64], wsT[:, g, :], zn[:, j, g * 64:(g + 1) * 64], True, True,
                                           [("wsT",), K("zn")], [("pf", bk)])
                            L.append(mix_mms)
                            L.append(lambda: tt("dve", mb[:].rearrange("p b (g c) -> p b g c", c=64),
                                                psF[:, mbk, :].rearrange("p (h g c) -> p h g c", h=NB, c=64),
                                                bsT[:, :].unsqueeze(1).unsqueeze(3).broadcast_to([128, NB, 4, 64]), ALU.add,
                                                [("pf", mbk), ("bsT",)], [K("mb")]))
                            L.append(lambda: tt("dve", mb[:], mb[:], guzU[:, i0:i0 + NB, :], ALU.mult, [K("mb")] + ku, [K("mb")]))
                            L.append(lambda: act(sqb[:], mb[:], AF.Square, [K("mb")], [K("sqb")]))
                            L.append(lambda: S.op("dve", lambda e, o=ssg[:, bt, :], a=sqb[:]: e.tensor_reduce(out=o, in_=a, axis=AX.X, op=ALU.add),
                                                  reads=[K("sqb")], writes=[("ssg", bt)]))
                            L.append(lambda: rstd_chain(ssg[:, bt, :], lgg[:, bt, :], rg[:, bt, :], 256.0, ("ssg", bt), ("lgg", bt), ("rg", bt)))
                            L.append(lambda: tt("dve", zc[:], mb[:], rg[:, bt, :].unsqueeze(2).broadcast_to([128, NB, 256]), ALU.mult,
                                                [K("mb"), ("rg", bt)], [K("zc")]))
                            L.append(lambda: tt("dve", gmn[:], zc[:], ggb[:, :].unsqueeze(1).broadcast_to([128, NB, 256]), ALU.mult,
                                                [K("zc"), ("ggb",)], [K("gmn")]))

                            def trs():
                                for j in range(NB):
                                    for c in range(2):
                                        q_ = j * 2 + c
                                        tr(psT[:, tb, toff + q_ * 128: toff + (q_ + 1) * 128], gmn[:, j, c * 128:(c + 1) * 128], ident[:],
                                           [K("gmn"), ("ident",)], [("pf", 6 + tb)])
                            L.append(trs)
                            L.append(lambda: cp("act" if bt % 2 == 0 else "dve",
                                                gmT[:, :, i0 * 128:(i0 + NB) * 128].rearrange("p c (j i) -> p c j i", i=128),
                                                psT[:, tb, toff:toff + NB * 256].rearrange("p (j c i) -> p c j i", c=2, i=128),
                                                [("pf", 6 + tb)], [("gmT", i0 // 4)]))
                            return L

                        for bp in range(0, NBT, NSET):
                            Ls = [gm_batch(bp + k_) for k_ in range(NSET)]
                            for fs in zip(*Ls):
                                for f_ in fs:
                                    f_()
                    S.barrier()
                    with contextlib.ExitStack() as ST:
                        wqkv = [alloc(ST, "wqkv%d" % i, [128, 3, 8, 128], BF16) for i in range(2)]
                        wt = [alloc(ST, "wt%d" % i, [128, 2, 3, 256], BF16) for i in range(2)]
                        kT = alloc(ST, "kT", [128, 4096], BF16)
                        vT = alloc(ST, "vT", [128, 4096], BF16)
                        qT = alloc(ST, "qT", [128, 2048], BF16)
                        vaug = [alloc(ST, "vaug%d" % i, [128, 32, 192], BF16) for i in range(2)]
                        acc = [alloc(ST, "acc%d" % i, [128, 2048], F32) for i in range(2)]
                        rec = alloc(ST, "rec", [128, 1, 512], F32)
                        lnd = alloc(ST, "lnd", [128, 1, 512], F32)
                        P2 = [alloc(ST, "P2_%d" % i, [128, 2, 512], BF16) for i in range(4)]
                        otmp = [alloc(ST, "otmp%d" % i, [128, 512], F32) for i in range(2)]

                        def load_pair_weights(p):
                            wb = p % 2
                            for k, col0 in enumerate((p * 128, 768 + p * 128, 1536 + p * 128)):
                                dma("pool", wqkv[wb][:, k], w_in_v[:, :, col0:col0 + 128], [], [("wqkv", wb, k)])
                            dma("pool", wt[wb][:], wtab_d[:, 2 * p:2 * p + 2], [], [("wt", wb)])

                        for vbi in range(2):
                            if s == 0:
                                cp("pool", vaug[vbi][:, 0:16, 64:128], hm[:, :].unsqueeze(1).broadcast_to([128, 16, 64]),
                                   [("hm",)], [("vaug1", vbi)])
                            else:
                                memset("pool", vaug[vbi][:, 0:16, 64:128], 1.0, [("vaug1", vbi)])
                            memset("pool", vaug[vbi][:, 16:32, 64:128], 1.0, [("vaug1", vbi)])
                        load_pair_weights(0)
                        pat_ctr = 0
                        bank_rr = [0]
                        tb_rr = [0]

                        def proj_bank():
                            bank_rr[0] += 1
                            return bank_rr[0] % 6

                        for p in range(DBG['pairs']):
                            wb = p % 2
                            if p + 1 < 6:
                                load_pair_weights(p + 1)
                            if s == 0 and p < 4:
                                precast_ffn(p)
                            if p == 1:
                                dma("pool", wo[:], w_out_v, [], [("wo",)])
                            ktiles = range(8) if s == 0 else range(4, 8)
                            if s == 1:
                                dma("sp", kT[:, 0:2048], kvs[p, 0], [("kvs", p, 0)], [("kT", t_) for t_ in range(4)])
                                dma("sp", vT[:, 0:2048], kvs[p, 1], [("kvs", p, 1)], [("vT", t_) for t_ in range(4)])
                            for (k, dst, name, tiles) in ((1, kT, "kT", ktiles), (2, vT, "vT", ktiles), (0, qT, "qT", range(4, 8))):
                                for t in tiles:
                                    bank = proj_bank()
                                    hk = [("hnT", 4 * t + j) for j in range(4)]
                                    for c in range(8):
                                        mm(psF[:, bank, :], wqkv[wb][:, k, c, :], hnT[:, c, t * 512:(t + 1) * 512], c == 0, c == 7,
                                           hk + [("wqkv", wb, k)], [("pf", bank)])
                                    if k == 0:
                                        tq = t - 4
                                        if evac_eng() == "act":
                                            act(qT[:, tq * 512:(tq + 1) * 512], psF[:, bank, :], AF.Copy, [("pf", bank)], [("qT", tq)], scale=0.125)
                                        else:
                                            ts1("dve", qT[:, tq * 512:(tq + 1) * 512], psF[:, bank, :], 0.125, ALU.mult, [("pf", bank)], [("qT", tq)])
                                    else:
                                        cp(evac_eng(), dst[:, t * 512:(t + 1) * 512], psF[:, bank, :], [("pf", bank)], [(name, t)])
                                if s == 0 and k in (1, 2):
                                    dma("sp", kvs[p, k - 1], dst[:, 2048:4096], [(name, t_) for t_ in range(4, 8)], [("kvs", p, k - 1)])
                            for pi, d in enumerate(PATTERNS):
                                nbd = 32 // d
                                hb = nbd // 2
                                vbi = pat_ctr % 2
                                pat_ctr += 1
                                vb = vaug[vbi]
                                vv = vT[:].rearrange("p (b i d) -> p d b i", d=d, i=128)
                                kv = kT[:].rearrange("p (b i d) -> p d b i", d=d, i=128)
                                qv = qT[:].rearrange("p (b i d) -> p d b i", d=d, i=128)

                                def tiles_of(r, b, d=d):
                                    lo = (b * 128 * d) // 512
                                    hi = (b * 128 * d + 127 * d + r) // 512
                                    return range(lo, hi + 1)

                                if d == 1:
                                    halo_groups = [[15]]
                                elif d == 4:
                                    halo_groups = [[3, 7, 11, 15]]
                                else:
                                    halo_groups = [list(range(0, 8)), list(range(8, 16))]
                                own_groups = [list(range(16, 24)), list(range(24, 32))]
                                vb4 = vb[:].rearrange("p n (a c) -> p n a c", c=64)
                                for grp in halo_groups + own_groups:
                                    tb = tb_rr[0] % 2
                                    tb_rr[0] += 1
                                    for j, idv in enumerate(grp):
                                        r, b = vid_inv(d, idv)
                                        tr(psT[:, tb, j * 128:(j + 1) * 128], vv[:, r, b, :], ident[:],
                                           [("vT", t) for t in tiles_of(r, b)] + [("ident",)], [("pf", 6 + tb)])
                                    n = len(grp)
                                    step = (grp[1] - grp[0]) if n > 1 else 1
                                    o_ap = vb4[:, grp[0]:grp[0] + step * (n - 1) + 1:step, 0:3:2, :]
                                    i_ap = psT[:, tb, 0:n * 128].rearrange("p (n a c) -> p n a c", a=2, c=64)
                                    wk = [("vaug", vbi, idv) for idv in grp]
                                    if s == 0 and grp[0] < 16:
                                        act(o_ap, i_ap, AF.Copy, [("pf", 6 + tb), ("hm",)], wk, scale=hm[:, 0:1])
                                    else:
                                        cp(evac_eng(), o_ap, i_ap, [("pf", 6 + tb)], wk)

                                if d == 1:
                                    groups = [[(0, 16 + 4 * g + j) for j in range(4)] for g in range(4)]
                                elif d == 4:
                                    groups = [[(r, 4 + j) for j in range(4)] for r in range(4)]
                                else:
                                    groups = [[(4 * g + j, 1) for j in range(4)] for g in range(4)]
                                units = [(gi, half) for gi in range(4) for half in range(2)]
                                ustate = {}
                                pu_rr = [0]
                                pt_rr = [0]
                                og_rr = [0]

                                def emit_qk(u, d=d, hb=hb, kv=kv, qv=qv, groups=groups, pi=pi, wb=wb):
                                    gi, half = units[u]
                                    sA = 2 * (pu_rr[0] % 3)
                                    pu_rr[0] += 1
                                    pt = pt_rr[0] % 4
                                    pt_rr[0] += 1
                                    for jj in range(2):
                                        r, b = groups[gi][2 * half + jj]
                                        qk = [("qT", t - 4) for t in tiles_of(r, b)]
                                        for which, kb_ in enumerate((b - 1, b)):
                                            kk = [("kT", t) for t in tiles_of(r, kb_)]
                                            col = jj * 256 + which * 128
                                            for hh in range(2):
                                                psl = slice(0, 64) if hh == 0 else slice(64, 128)
                                                mm(psF[:, sA + hh, col:col + 128], kv[psl, r, kb_, :], qv[psl, r, b - hb, :], True, True,
                                                   kk + qk, [("pf", sA + hh)])
                                    act(P2[pt][:], psF[:, sA:sA + 2, :], AF.Exp, [("pf", sA), ("pf", sA + 1)], [("P2", pt)])
                                    pv4 = P2[pt][:].rearrange("p h (j k) -> p h j k", k=256)
                                    tt("dve", pv4, pv4, wt[wb][:, :, pi, :].unsqueeze(2).broadcast_to([128, 2, 2, 256]), ALU.mult,
                                       [("P2", pt), ("wt", wb)], [("P2", pt)])
                                    ustate[u] = pt

                                def emit_pv(u, d=d, groups=groups, vb=vb, vbi=vbi, pi=pi):
                                    gi, half = units[u]
                                    pt = ustate.pop(u)
                                    ob = 6
                                    for jj in range(2):
                                        r, b = groups[gi][2 * half + jj]
                                        q = 2 * half + jj
                                        for which, kb_ in enumerate((b - 1, b)):
                                            idk = vid(d, r, kb_)
                                            for hh in range(2):
                                                vcols = slice(0, 128) if hh == 0 else slice(64, 192)
                                                mm(psF[:, ob + hh, q * 128:(q + 1) * 128], vb[:, idk, vcols],
                                                   P2[pt][:, hh, jj * 256 + which * 128: jj * 256 + (which + 1) * 128],
                                                   which == 0, which == 1,
                                                   [("vaug", vbi, idk), ("vaug1", vbi), ("P2", pt)], [("pf", ob + hh)])
                                    if half == 1:
                                        for hh in range(2):
                                            accv = acc[hh][:].rearrange("p (b i d) -> p d b i", d=d, i=128)
                                            if d == 1:
                                                dst = accv[:, 0, 4 * gi:4 * gi + 4, :]
                                            elif d == 4:
                                                dst = accv[:, gi, 0:4, :]
                                            else:
                                                dst = accv[:, 4 * gi:4 * gi + 4, 0, :]
                                            src = psF[:, ob + hh, :].rearrange("p (q i) -> p q i", i=128)
                                            if pi == 0:
                                                cp("act", dst, src, [("pf", ob + hh)], [("acc", hh)])
                                            else:
                                                oi_ = og_rr[0] % 2
                                                og_rr[0] += 1
                                                cp("act", otmp[oi_][:].rearrange("p (q i) -> p q i", i=128), src, [("pf", ob + hh)], [("otmp", oi_)])
                                                tt("pool", dst, dst, otmp[oi_][:].rearrange("p (q i) -> p q i", i=128), ALU.add,
                                                   [("otmp", oi_), ("acc", hh)], [("acc", hh)])

                                SKEW = 2
                                nu = len(units)
                                for u in range(min(SKEW, nu)):
                                    emit_qk(u)
                                for u in range(nu):
                                    if u + SKEW < nu:
                                        emit_qk(u + SKEW)
                                    emit_pv(u)
                            for hh in range(2):
                                nps = slice(0, 64) if hh == 0 else slice(64, 128)
                                dps = slice(64, 128) if hh == 0 else slice(0, 64)
                                for ch in range(4):
                                    cols = slice(ch * 512, (ch + 1) * 512)
                                    rb = 0
                                    act(lnd[dps, rb, :], acc[hh][dps, cols], AF.Ln, [("acc", hh)], [("lnd", hh, rb)])
                                    act(rec[nps, rb, :], lnd[dps, rb, :], AF.Exp, [("lnd", hh, rb)], [("rec", hh, rb)], scale=-1.0)
                                    stt("dve", attnT[nps, p, cols], acc[hh][nps, cols], gacol[nps, p:p + 1], rec[nps, rb, :],
                                        ALU.mult, ALU.mult, [("acc", hh), ("gacol",), ("rec", hh, rb)], [("attnT", p, hh, ch)])
                    S.barrier()
                S.barrier()
                with contextlib.ExitStack() as SC:
                    NT = DBG['tilesC'] if DBG['phaseC'] else 0
                    g2bc = alloc(SC, "g2bc", [128, 1024], F32)
                    gfbc = alloc(SC, "gfbc", [128, 1024], F32)
                    ones_bb = alloc(SC, "ones_bb", [128, 128], BF16)
                    sq = alloc(SC, "sq", [128, 6, 512], BF16)
                    attnN = alloc(SC, "attnN", [128, 6, 512], BF16)
                    lnra = alloc(SC, "lnra", [128, 512], F32)
                    rabc = alloc(SC, "rabc", [128, 512], F32)
                    h1 = [alloc(SC, "h1_%d" % i, [128, 4, 1024], F32) for i in range(2)]
                    hn2 = [alloc(SC, "hn2_%d" % i, [128, 1024], BF16) for i in range(2)]
                    hn2T = [alloc(SC, "hn2T%d" % i, [128, 8, 512], BF16) for i in range(2)]
                    aT = alloc(SC, "aT", [128, 32, 512], BF16)
                    rl = [alloc(SC, "rl%d" % i, [128, 512], F32) for i in range(3)]
                    W1b = [alloc(SC, "W1b%d" % i, [128, 8, 512], BF16) for i in range(2)]
                    W2b = [alloc(SC, "W2b%d" % i, [128, 4, 512], BF16) for i in range(3)]
                    ot = [alloc(SC, "ot%d" % i, [128, 1024], F32) for i in range(2)]
                    junkc = alloc(SC, "junkc", [128, 1024], BF16)
                    ss2 = alloc(SC, "ss2", [128, 16], F32)
                    ln2 = alloc(SC, "ln2", [128, 16], F32)
                    r2 = alloc(SC, "r2", [128, 16], F32)
                    ssf = alloc(SC, "ssf", [128, 16, 2], F32)
                    ssft = alloc(SC, "ssft", [128, 16], F32)
                    lnf = alloc(SC, "lnf", [128, 16], F32)
                    rf = alloc(SC, "rf", [128, 16], F32)
                    dma("sp", g2bc[:], g2_d, [], [("g2bc",)])
                    dma("sp", gfbc[:], gf_d, [], [("gfbc",)])
                    memset("pool", ones_bb[:], 1.0, [("ones_bb",)])
                    cst = dict(w1=0, w2=0, rl=0, ot=0)
                    w1_slot = {}
                    w2_slot = {}
                    units2 = [(h, g) for h in range(2) for g in range(8)]

                    def issue_w1(t, g):
                        sl = cst["w1"] % 2
                        cst["w1"] += 1
                        dma("sp", W1b[sl][:], w1s[:, g], [("w1s", g)], [("W1b", sl)])
                        w1_slot[(t, g)] = sl

                    def issue_w2(t, u):
                        h, g = units2[u]
                        sl = cst["w2"] % 3
                        cst["w2"] += 1
                        dma("sp", W2b[sl][:], w2s[:, h, g], [("w2s", h, g)], [("W2b", sl)])
                        w2_slot[(t, u)] = sl

                    def akeys(t):
                        return [("attnT", p_, h_, t) for p_ in range(6) for h_ in range(2)]

                    wprep_done = set()

                    def Wprep_c(t, bank):
                        if t in wprep_done:
                            return
                        wprep_done.add(t)
                        T0 = t * 512
                        eng_ = "dve" if t < 2 else "pool"
                        tt(eng_, sq[:], attnT[:, :, T0:T0 + 512], attnT[:, :, T0:T0 + 512], ALU.mult, akeys(t), [("sq",)])
                        for c in range(6):
                            mm(psF[:, bank, :], ones_bb[:], sq[:, c, :], c == 0, c == 5, [("sq",), ("ones_bb",)], [("pf", bank)])
                        act(lnra[:], psF[:, bank, :], AF.Ln, [("pf", bank)], [("lnra",)], scale=1.0 / 768, bias=EPS)
                        act(rabc[:], lnra[:], AF.Exp, [("lnra",)], [("rabc",)], scale=-0.5)
                        tt(eng_, attnN[:], attnT[:, :, T0:T0 + 512], rabc[:, :].unsqueeze(1).broadcast_to([128, 6, 512]), ALU.mult,
                           akeys(t) + [("rabc",)], [("attnN",)])

                    def Wprep(t):
                        T0 = t * 512
                        Wprep_c(t, 0)
                        for blk in range(4):
                            row0 = s * 2048 + T0 + blk * 128
                            dma("sp", h1[t % 2][:, blk, :], x_own[row0:row0 + 128, :], [], [("h1", t % 2, blk)])

                    def WP1(t, blk):
                        c0 = t * 512 + blk * 128
                        bb = 2 * (blk % 2)
                        for half in range(2):
                            for c in range(8):
                                lhs = attnN[:, c, blk * 128:(blk + 1) * 128] if c < 6 else gmT[:, c - 6, c0:c0 + 128]
                                mm(psF[:, bb + half, :], lhs, wo[:, c, half * 512:(half + 1) * 512], c == 0, c == 7,
                                   [("attnN",), ("gmT", t), ("wo",)], [("pf", bb + half)])
                        hv = h1[t % 2][:, blk, :].rearrange("p (h n) -> p h n", n=512)
                        tt("dve", hv, hv, psF[:, bb:bb + 2, :], ALU.add,
                           [("pf", bb), ("pf", bb + 1), ("h1", t % 2, blk)], [("h1", t % 2, blk)])

                    def WA(t, blk):
                        bi = t * 4 + blk
                        act(junkc[:], h1[t % 2][:, blk, :], AF.Square, [("h1", t % 2, blk)], [("junkc",), ("ss2", bi)],
                            accum_out=ss2[:, bi:bi + 1])
                        rstd_chain(ss2[:, bi:bi + 1], ln2[:, bi:bi + 1], r2[:, bi:bi + 1], 1024.0, ("ss2", bi), ("ln2", bi), ("r2", bi))
                        hb_ = blk % 2
                        stt("dve", hn2[hb_][:], h1[t % 2][:, blk, :], r2[:, bi:bi + 1], g2bc[:], ALU.mult, ALU.mult,
                            [("h1", t % 2, blk), ("r2", bi), ("g2bc",)], [("hn2", hb_)])

                    def WP2(t, blk):
                        hb_ = blk % 2
                        tb = blk % 2
                        for c in range(8):
                            tr(psT[:, tb, c * 128:(c + 1) * 128], hn2[hb_][:, c * 128:(c + 1) * 128], ident[:],
                               [("hn2", hb_), ("ident",)], pTkeys(tb))
                        cp("dve", hn2T[t % 2][:, :, blk * 128:(blk + 1) * 128],
                           psT[:, tb, :].rearrange("p (c i) -> p c i", i=128), pTkeys(tb), [("hn2T", t % 2, blk)])

                    def ff1_group(t, g):
                        hkeys = [("hn2T", t % 2, b_) for b_ in range(4)]
                        cur = w1_slot[(t, g)]
                        for cc in range(4):
                            ffc = 4 * g + cc
                            bank = 4 + ffc % 2
                            for c in range(8):
                                mm(psF[:, bank, :], W1b[cur][:, c, cc * 128:(cc + 1) * 128], hn2T[t % 2][:, c, :], c == 0, c == 7,
                                   hkeys + [("W1b", cur)], [("pf", bank)])
                            ri = cst["rl"] % 3
                            cst["rl"] += 1
                            act(rl[ri][:], psF[:, bank, :], AF.Relu, [("pf", bank)], [("rl", ri)])
                            tt("pool", aT[:, ffc, :], rl[ri][:], rl[ri][:], ALU.mult, [("rl", ri)], [("aT", ffc)])

                    def ff2(t):
                        for u, (h, g) in enumerate(units2):
                            if u + 2 < len(units2):
                                issue_w2(t, u + 2)
                            if u == 12 and t + 1 < NT:
                                issue_w1(t + 1, 0)
                            if u == 3 and t + 2 < NT:
                                Wprep_c(t + 2, 4)
                            sl = w2_slot[(t, u)]
                            for blk in range(4):
                                for cc in range(4):
                                    ffc = 4 * g + cc
                                    mm(psF[:, blk, :], aT[:, ffc, blk * 128:(blk + 1) * 128], W2b[sl][:, cc, :],
                                       g == 0 and cc == 0, g == 7 and cc == 3,
                                       [("aT", ffc), ("W2b", sl)], [("pf", blk)])
                            if g == 7:
                                for blk in range(4):
                                    bi = t * 4 + blk
                                    hsl = h1[t % 2][:, blk, h * 512:(h + 1) * 512]
                                    tt("dve", hsl, hsl, psF[:, blk, :], ALU.add, [("pf", blk), ("h1", t % 2, blk)], [("h1", t % 2, blk)])
                                    act(junkc[:, 0:512], hsl, AF.Square, [("h1", t % 2, blk)],
                                        [("junkc",), ("ssf", bi, h)], accum_out=ssf[:, bi, h:h + 1])

                    def final(t):
                        for blk in range(4):
                            bi = t * 4 + blk
                            tt("dve", ssft[:, bi:bi + 1], ssf[:, bi, 0:1], ssf[:, bi, 1:2], ALU.add,
                               [("ssf", bi, 0), ("ssf", bi, 1)], [("ssft", bi)])
                            rstd_chain(ssft[:, bi:bi + 1], lnf[:, bi:bi + 1], rf[:, bi:bi + 1], 1024.0, ("ssft", bi), ("lnf", bi), ("rf", bi))
                            oi = cst["ot"] % 2
                            cst["ot"] += 1
                            stt("dve", ot[oi][:], h1[t % 2][:, blk, :], rf[:, bi:bi + 1], gfbc[:], ALU.mult, ALU.mult,
                                [("h1", t % 2, blk), ("rf", bi), ("gfbc",)], [("ot", oi)])
                            row0 = s * 2048 + t * 512 + blk * 128
                            out_ops.append(dma("sp", out_d[row0:row0 + 128, :], ot[oi][:], [("ot", oi)], [("out", row0)]))

                    if NT > 0:
                        issue_w1(0, 0)
                        Wprep(0)
                        WP1(0, 0)
                        WP1(0, 1)
                        WA(0, 0)
                        WA(0, 1)
                        WP2(0, 0)
                        WP1(0, 2)
                        WP2(0, 1)
                        WP1(0, 3)
                        WA(0, 2)
                        WA(0, 3)
                        WP2(0, 2)
                        WP2(0, 3)
                    for t in range(NT):
                        nxt = t + 1 < NT
                        for g in range(8):
                            if g + 1 < 8:
                                issue_w1(t, g + 1)
                            if g == 6:
                                issue_w2(t, 0)
                                issue_w2(t, 1)
                            if nxt:
                                if g == 0:
                                    Wprep(t + 1)
                                if g < 4:
                                    WP1(t + 1, g)
                                if 1 <= g < 5:
                                    WA(t + 1, g - 1)
                                if 2 <= g < 6:
                                    WP2(t + 1, g - 2)
                            ff1_group(t, g)
                        ff2(t)
                        final(t)
                S.barrier()
        S.emit(final_wait_ops=out_ops)
    return nc, S


_CACHE = {}


def _host_consts():
    if "c" not in _CACHE:
        _CACHE["c"] = dict(
            wtab=make_wtab(),
            tri=np.triu(np.ones((128, 128), dtype=np.float32)),
            ident=np.eye(128, dtype=np.float32),
        )
    return _CACHE["c"]


def kernel(x, norm1_g, w_in, sgu_ln_g, sgu_ln_b, sgu_w, sgu_b, attn_out_g, gmlp_out_g,
           w_out, norm2_g, w_ff1, w_ff2, final_norm_g):
    f = lambda a: np.ascontiguousarray(np.asarray(a, dtype=np.float32))
    x = f(x)
    consts = _host_consts()
    bc = lambda v, n: np.ascontiguousarray(np.broadcast_to(f(v).reshape(1, n), (128, n)))
    shared = dict(
        w_in=f(w_in)[0], w_out=f(w_out)[0], w_ff1=f(w_ff1)[0], w_ff2=f(w_ff2)[0],
        g1_bc=bc(np.asarray(norm1_g)[0], 1024), g2_bc=bc(np.asarray(norm2_g)[0], 1024), gf_bc=bc(final_norm_g, 1024),
        ga_col=np.ascontiguousarray(f(attn_out_g)[0].reshape(6, 128).T),
        gg_bc=bc(np.asarray(gmlp_out_g)[0], 256),
        lng_bc=bc(np.asarray(sgu_ln_g)[0].reshape(-1), 256), lnb_bc=bc(np.asarray(sgu_ln_b)[0].reshape(-1), 256),
        bsT=np.ascontiguousarray(f(sgu_b)[0].T),
        wsT=np.ascontiguousarray(f(sgu_w)[0].transpose(2, 0, 1)),
        tri=consts["tri"], wtab=consts["wtab"], ident=consts["ident"],
    )
    in_maps = []
    for c in range(N_CORES):
        b, half = divmod(c, 2)
        m = dict(shared)
        m["x_own"] = np.ascontiguousarray(x[b, half * 4096:(half + 1) * 4096])
        if half == 1:
            m["x_halo"] = np.ascontiguousarray(x[b, 2048:4096])
            m["hmask"] = np.ones((128, 64), dtype=np.float32)
        else:
            m["x_halo"] = np.zeros((2048, 1024), dtype=np.float32)
            m["hmask"] = np.zeros((128, 64), dtype=np.float32)
        in_maps.append(m)
    if "nc" not in _CACHE:
        _CACHE["nc"] = build_program()
    nc, _ = _CACHE["nc"]
    res = run_bass_kernel_spmd(nc, in_maps, core_ids=list(range(N_CORES)))
    outp = np.empty((4, 8192, 1024), dtype=np.float32)
    for c in range(N_CORES):
        b, half = divmod(c, 2)
        outp[b, half * 4096:(half + 1) * 4096] = res.results[c]["out"]
    return outp
```

```python
import math
import contextlib
import numpy as np
import concourse.bass as bass
import concourse.mybir as mybir
from concourse.bass_utils import run_bass_kernel_spmd

F32 = mybir.dt.float32
BF16 = mybir.dt.bfloat16
AF = mybir.ActivationFunctionType
ALU = mybir.AluOpType
AX = mybir.AxisListType
EPS = 1e-6
PATTERNS = (1, 4, 16)
N_CORES = 8
DBG = dict(spans=2, pairs=6, phaseC=True, gmlp=True, tilesC=4)


class Sched:
    ENGS = ("pe", "act", "dve", "pool", "sp")

    def __init__(self, nc):
        self.nc = nc
        self.eng = {"pe": nc.tensor, "act": nc.scalar, "dve": nc.vector, "pool": nc.gpsimd, "sp": nc.sync}
        self.ops = []
        self.last_w = {}
        self.readers = {}
        self.last_by_eng = {}
        self.dma_since_barrier = []
        self.pending_barrier = {}

    def barrier(self):
        deps = set(self.last_by_eng.values()) | set(self.dma_since_barrier)
        self.dma_since_barrier = []
        for e in self.ENGS:
            self.pending_barrier.setdefault(e, set()).update(deps)

    def op(self, eng, fn, reads=(), writes=(), dma=False):
        idx = len(self.ops)
        deps = set()
        raw = set()
        for k in reads:
            w = self.last_w.get(k)
            if w is not None:
                deps.add(w)
                raw.add(w)
        for k in writes:
            w = self.last_w.get(k)
            if w is not None:
                deps.add(w)
            for r in self.readers.get(k, {}).values():
                deps.add(r)
        pb = self.pending_barrier.pop(eng, None)
        if pb:
            deps |= pb
            raw |= pb
        deps.discard(idx)
        self.ops.append(dict(eng=eng, fn=fn, deps=deps, raw=raw, dma=dma))
        for k in reads:
            rd = self.readers.setdefault(k, {})
            if dma:
                rd[("dma", idx)] = idx
            else:
                rd[eng] = idx
        for k in writes:
            self.last_w[k] = idx
            self.readers[k] = {}
        if dma:
            self.dma_since_barrier.append(idx)
        else:
            self.last_by_eng[eng] = idx
        return idx

    def _needs_sem(self, o, d, od):
        if od["dma"]:
            return True
        if od["eng"] != o["eng"]:
            return True
        if o["dma"]:
            return True
        return o["eng"] != "pe"

    def emit(self, final_wait_ops=()):
        nc = self.nc
        ops = self.ops
        n = len(ops)
        need = [False] * n
        for o in ops:
            for d in o["deps"]:
                if self._needs_sem(o, d, ops[d]):
                    need[d] = True
        for d in final_wait_ops:
            need[d] = True
        with contextlib.ExitStack() as st:
            esem = {e: st.enter_context(nc.semaphore("s_" + e)) for e in self.ENGS}
            NDMA = 40
            dsem = [st.enter_context(nc.semaphore("d%d" % i)) for i in range(NDMA)]
            slots_of = {"sp": list(range(0, 24)), "pool": list(range(24, 40))}
            slot_rr = {"sp": 0, "pool": 0}
            ecount = {e: 0 for e in self.ENGS}
            dcount = [0] * NDMA
            dnext = 0
            sig = [None] * n
            waited = {e: {} for e in self.ENGS}
            nwaits = 0

            def do_wait(e, s):
                nonlocal nwaits
                sem, val, key = s
                if waited[e].get(key, -1) >= val:
                    return
                self.eng[e].wait_ge(sem, val)
                waited[e][key] = val
                nwaits += 1

            for i, o in enumerate(ops):
                e = o["eng"]
                for d in sorted(o["deps"]):
                    if need[d] and self._needs_sem(o, d, ops[d]):
                        do_wait(e, sig[d])
                if o["dma"]:
                    slot = slots_of[e][slot_rr[e] % len(slots_of[e])]
                    slot_rr[e] += 1
                    dnext += 1
                    if dcount[slot] > 0:
                        do_wait(e, (dsem[slot], dcount[slot], ("d", slot)))
                    inst = o["fn"](self.eng[e])
                    dcount[slot] += 16
                    inst.then_inc(dsem[slot], 16)
                    sig[i] = (dsem[slot], dcount[slot], ("d", slot))
                else:
                    inst = o["fn"](self.eng[e])
                    if need[i]:
                        ecount[e] += 1
                        inst.then_inc(esem[e], 1)
                        sig[i] = (esem[e], ecount[e], ("e", e))
            for d in final_wait_ops:
                do_wait("sp", sig[d])
            self.stats = dict(n_ops=n, ecount=dict(ecount), ndma=dnext, nwaits=nwaits)


def alibi_slopes(n):
    def pow2_slopes(m):
        start = 2.0 ** (-8.0 / m)
        return [start ** (i + 1) for i in range(m)]
    if math.log2(n).is_integer():
        s = pow2_slopes(n)
    else:
        c = 2 ** int(math.floor(math.log2(n)))
        s = pow2_slopes(c) + pow2_slopes(2 * c)[0::2][: n - c]
    return np.asarray(s, dtype=np.float32)


def make_wtab():
    sl = alibi_slopes(12).astype(np.float64)
    i = np.arange(128)[:, None]
    j = np.arange(128)[None, :]
    tab = np.zeros((128, 12, 3, 256), dtype=np.float64)
    for h in range(12):
        for pi, d in enumerate(PATTERNS):
            steps_prev = j + 128 - i
            steps_cur = j - i
            tab[:, h, pi, 0:128] = np.where(j <= i, np.exp(-sl[h] * d * np.maximum(steps_prev, 0)), 0.0)
            tab[:, h, pi, 128:256] = np.where(j >= i, np.exp(-sl[h] * d * np.maximum(steps_cur, 0)), 0.0)
    return tab.astype(np.float32)


def vid(d, r, b):
    hb = (32 // d) // 2
    return (b // hb) * 16 + r * hb + (b % hb)


def vid_inv(d, i):
    hb = (32 // d) // 2
    sec, w = divmod(i, 16)
    r, bb = divmod(w, hb)
    return r, sec * hb + bb


def build_program():
    nc = bass.Bass("TRN2", target_bir_lowering=False)

    def din(name, shape):
        return nc.dram_tensor(name, shape, F32, kind="ExternalInput").ap()

    x_own = din("x_own", [4096, 1024])
    x_halo = din("x_halo", [2048, 1024])
    hmask_d = din("hmask", [128, 64])
    w_in = din("w_in", [1024, 2816])
    w_out = din("w_out", [1024, 1024])
    w_ff1 = din("w_ff1", [1024, 4096])
    w_ff2 = din("w_ff2", [4096, 1024])
    g1_d = din("g1_bc", [128, 1024])
    g2_d = din("g2_bc", [128, 1024])
    gf_d = din("gf_bc", [128, 1024])
    ga_d = din("ga_col", [128, 6])
    gg_d = din("gg_bc", [128, 256])
    lng_d = din("lng_bc", [128, 256])
    lnb_d = din("lnb_bc", [128, 256])
    bs_d = din("bsT", [128, 4])
    ws_d = din("wsT", [128, 4, 128])
    tri_d = din("tri", [128, 128])
    wtab_d = din("wtab", [128, 12, 3, 256])
    ident_d = din("ident", [128, 128])
    out_d = nc.dram_tensor("out", [4096, 1024], F32, kind="ExternalOutput").ap()
    w1s = nc.dram_tensor("w1s", [128, 8, 8, 512], BF16, kind="Internal").ap()
    w2s = nc.dram_tensor("w2s", [128, 2, 8, 4, 512], BF16, kind="Internal").ap()
    kvs = nc.dram_tensor("kvs", [6, 2, 128, 2048], BF16, kind="Internal").ap()

    w_in_v = w_in.rearrange("(c p) n -> p c n", p=128)
    w_out_v = w_out.rearrange("(c p) n -> p c n", p=128)
    w_ff1_v = w_ff1.rearrange("(c p) n -> p c n", p=128)
    w_ff2_v = w_ff2.rearrange("(c p) n -> p c n", p=128)

    S = Sched(nc)
    out_ops = []
    uid = [0]

    def alloc(st, name, shape, dt):
        uid[0] += 1
        return st.enter_context(nc.sbuf_tensor("%s_%d" % (name, uid[0]), shape, dt))

    def dma(q, out, in_, reads, writes):
        return S.op(q, lambda e, out=out, in_=in_: e.dma_start(out=out, in_=in_), reads=reads, writes=writes, dma=True)

    def mm(out, lhsT, rhs, start, stop, reads, writes):
        S.op("pe", lambda e, out=out, lhsT=lhsT, rhs=rhs, start=start, stop=stop:
             e.matmul(out, lhsT=lhsT, rhs=rhs, start=start, stop=stop), reads=reads, writes=writes)

    def tr(out, in_, ident, reads, writes):
        S.op("pe", lambda e, out=out, in_=in_, ident=ident: e.transpose(out, in_, ident), reads=reads, writes=writes)

    def act(out, in_, func, reads, writes, **kw):
        S.op("act", lambda e, out=out, in_=in_, func=func, kw=kw: e.activation(out=out, in_=in_, func=func, **kw),
             reads=reads, writes=writes)

    def tt(eng, out, in0, in1, op, reads, writes):
        S.op(eng, lambda e, out=out, in0=in0, in1=in1, op=op: e.tensor_tensor(out=out, in0=in0, in1=in1, op=op),
             reads=reads, writes=writes)

    def stt(eng, out, in0, scalar, in1, op0, op1, reads, writes):
        S.op(eng, lambda e, out=out, in0=in0, scalar=scalar, in1=in1, op0=op0, op1=op1:
             e.scalar_tensor_tensor(out=out, in0=in0, scalar=scalar, in1=in1, op0=op0, op1=op1), reads=reads, writes=writes)

    def ts1(eng, out, in0, scalar, op, reads, writes):
        S.op(eng, lambda e, out=out, in0=in0, scalar=scalar, op=op:
             e.tensor_scalar(out=out, in0=in0, scalar1=scalar, scalar2=None, op0=op), reads=reads, writes=writes)

    def cp(eng, out, in_, reads, writes):
        if eng == "act":
            S.op("act", lambda e, out=out, in_=in_: e.copy(out=out, in_=in_), reads=reads, writes=writes)
        else:
            S.op(eng, lambda e, out=out, in_=in_: e.tensor_copy(out=out, in_=in_), reads=reads, writes=writes)

    def memset(eng, ap, val, writes):
        S.op(eng, lambda e, ap=ap, val=val: e.memset(ap, val), writes=writes)

    def rstd_chain(ss_ap, ln_ap, rs_ap, n, kss, kln, krs):
        act(ln_ap, ss_ap, AF.Ln, [kss], [kln], scale=1.0 / n, bias=EPS)
        act(rs_ap, ln_ap, AF.Exp, [kln], [krs], scale=-0.5)

    evac_rr = [0]

    def evac_eng():
        evac_rr[0] += 1
        return "act" if evac_rr[0] % 2 else "dve"

    with contextlib.ExitStack() as G:
        psF = G.enter_context(nc.psum_tensor("ps8", [128, 8, 512], F32))
        psT = psF[:, 6:8, :].bitcast(BF16)
        ident = alloc(G, "ident", [128, 128], BF16)
        hm = alloc(G, "hm", [128, 64], F32)
        gacol = alloc(G, "gacol", [128, 6], F32)
        onesb = alloc(G, "onesb", [128, 1], BF16)
        dma("pool", ident[:], ident_d, [], [("ident",)])
        dma("sp", hm[:], hmask_d, [], [("hm",)])
        dma("sp", gacol[:], ga_d, [], [("gacol",)])
        memset("pool", onesb[:], 1.0, [("onesb",)])

        def pTkeys(tb):
            return [("pf", 6 + tb)]

        def precast_ffn(part):
            jobs = [("w1", g) for g in range(8)] + [("w2", h, g) for h in range(2) for g in range(8)]
            for job in jobs[part * 6:(part + 1) * 6]:
                if job[0] == "w1":
                    g = job[1]
                    dma("pool", w1s[:, g], w_ff1_v[:, :, g * 512:(g + 1) * 512], [], [("w1s", g)])
                else:
                    _, h, g = job
                    dma("pool", w2s[:, h, g], w_ff2_v[:, 4 * g:4 * g + 4, h * 512:(h + 1) * 512], [], [("w2s", h, g)])

        for s in range(DBG['spans']):
            with contextlib.ExitStack() as SP:
                attnT = alloc(SP, "attnT", [128, 6, 2048], BF16)
                gmT = alloc(SP, "gmT", [128, 2, 2048], BF16)
                wo = alloc(SP, "wo", [128, 8, 1024], BF16)
                with contextlib.ExitStack() as SAB:
                    hnT = alloc(SAB, "hnT", [128, 8, 4096], BF16)
                    with contextlib.ExitStack() as SA:
                        NXB = 3
                        g1bc = alloc(SA, "g1bc", [128, 1024], F32)
                        xt = [alloc(SA, "xt%d" % i, [128, 2, 1024], F32) for i in range(NXB)]
                        xs = [alloc(SA, "xs%d" % i, [128, 1024], BF16) for i in range(4)]
                        junk = alloc(SA, "junkA", [128, 1024], BF16)
                        ssA = alloc(SA, "ssA", [128, 32], F32)
                        lnA = alloc(SA, "lnA", [128, 32], F32)
                        rsA = alloc(SA, "rsA", [128, 32], F32)
                        dma("sp", g1bc[:], g1_d, [], [("g1bc",)])
                        xctr = [0]
                        xbuf_of = {}

                        def A1(kb):
                            if kb % 2 == 0:
                                if kb < 16:
                                    src = (x_halo if s == 0 else x_own)[kb * 128:(kb + 2) * 128, :]
                                else:
                                    r0 = s * 2048 + (kb - 16) * 128
                                    src = x_own[r0:r0 + 256, :]
                                bq = xctr[0] % NXB
                                xctr[0] += 1
                                xbuf_of[kb] = bq
                                xbuf_of[kb + 1] = bq
                                dma("sp", xt[bq][:], src.rearrange("(j p) f -> p j f", p=128), [], [("xt", bq, 0), ("xt", bq, 1)])
                            if kb % 2 == 1:
                                return
                            bq = xbuf_of[kb]
                            for j in range(2):
                                act(junk[:], xt[bq][:, j, :], AF.Square, [("xt", bq, j)], [("junkA",), ("ssA", kb + j)],
                                    accum_out=ssA[:, kb + j:kb + j + 1])
                            act(lnA[:, kb:kb + 2], ssA[:, kb:kb + 2], AF.Ln, [("ssA", kb), ("ssA", kb + 1)], [("lnA", kb)],
                                scale=1.0 / 1024, bias=EPS)
                            act(rsA[:, kb:kb + 2], lnA[:, kb:kb + 2], AF.Exp, [("lnA", kb)], [("rsA", kb)], scale=-0.5)
                            for j in range(2):
                                b = (kb + j) % 4
                                stt("dve", xs[b][:], xt[bq][:, j, :], rsA[:, kb + j:kb + j + 1], g1bc[:],
                                    ALU.mult, ALU.mult, [("xt", bq, j), ("rsA", kb), ("g1bc",)], [("xs", b)])

                        def A2(kb):
                            b = kb % 4
                            tb = kb % 2
                            for c in range(8):
                                tr(psT[:, tb, c * 128:(c + 1) * 128], xs[b][:, c * 128:(c + 1) * 128], ident[:],
                                   [("xs", b), ("ident",)], pTkeys(tb))
                            cp(evac_eng(), hnT[:, :, kb * 128:(kb + 1) * 128],
                               psT[:, tb, :].rearrange("p (c i) -> p c i", i=128), pTkeys(tb), [("hnT", kb)])

                        SKA = 2
                        kbs = list(range(32)) if s == 0 else list(range(16, 32))
                        for ii in range(len(kbs) + SKA):
                            if ii < len(kbs):
                                A1(kbs[ii])
                            if ii - SKA >= 0:
                                A2(kbs[ii - SKA])
                    S.barrier()
                    with contextlib.ExitStack() as SG:
                        wuz = alloc(SG, "wuz", [128, 8, 512], BF16)
                        wsT = alloc(SG, "wsT", [128, 4, 128], BF16)
                        wsf = alloc(SG, "wsf", [128, 4, 128], F32)
                        tri = alloc(SG, "tri", [128, 128], F32)
                        lng = alloc(SG, "lng", [128, 256], F32)
                        lnb = alloc(SG, "lnb", [128, 256], F32)
                        ggb = alloc(SG, "ggb", [128, 256], F32)
                        bsT = alloc(SG, "bsT", [128, 4], F32)
                        guzU = alloc(SG, "guzU", [128, 16, 256], F32)
                        guzZ = alloc(SG, "guzZ", [128, 16, 256], F32)
                        NB = 2
                        NSET = 4
                        NBT = (16 // NB) if DBG['gmlp'] else 0
                        sqb_ = [alloc(SG, "sqb%d" % i, [128, NB, 256], F32) for i in range(NSET)]
                        zc_ = [alloc(SG, "zc%d" % i, [128, NB, 256], F32) for i in range(NSET)]
                        zc2_ = [alloc(SG, "zc2%d" % i, [128, NB, 256], F32) for i in range(NSET)]
                        mb_ = [alloc(SG, "mb%d" % i, [128, NB, 256], F32) for i in range(NSET)]
                        zn_ = [alloc(SG, "zn%d" % i, [128, NB, 256], BF16) for i in range(NSET)]
                        gmn_ = [alloc(SG, "gmn%d" % i, [128, NB, 256], BF16) for i in range(NSET)]
                        msum = alloc(SG, "msum", [128, 8, 8], F32)
                        qsum = alloc(SG, "qsum", [128, 8, 8], F32)
                        mean = alloc(SG, "mean", [128, 8, 8], F32)
                        m2 = alloc(SG, "m2", [128, 8, 8], F32)
                        var = alloc(SG, "var", [128, 8, 8], F32)
                        lnv = alloc(SG, "lnv", [128, 8, 8], F32)
                        rsl = alloc(SG, "rsl", [128, 8, 8], F32)
                        ssg = alloc(SG, "ssg", [128, 8, 2], F32)
                        lgg = alloc(SG, "lgg", [128, 8, 2], F32)
                        rg = alloc(SG, "rg", [128, 8, 2], F32)
                        dma("pool", wuz[:], w_in_v[:, :, 2304:2816], [], [("wuz",)])
                        dma("sp", wsf[:], ws_d, [], [("wsf",)])
                        dma("sp", tri[:], tri_d, [], [("tri",)])
                        dma("sp", lng[:], lng_d, [], [("lng",)])
                        dma("sp", lnb[:], lnb_d, [], [("lnb",)])
                        dma("sp", ggb[:], gg_d, [], [("ggb",)])
                        dma("sp", bsT[:], bs_d, [], [("bsT",)])
                        tt("dve", wsT[:], wsf[:], tri[:].unsqueeze(1).broadcast_to([128, 4, 128]), ALU.mult,
                           [("wsf",), ("tri",)], [("wsT",)])
                        for i in range(16 if DBG['gmlp'] else 0):
                            kb = 16 + i
                            bank = 4 + i % 2
                            for c in range(8):
                                mm(psF[:, bank, :], hnT[:, c, kb * 128:(kb + 1) * 128], wuz[:, c, :], c == 0, c == 7,
                                   [("hnT", kb), ("wuz",)], [("pf", bank)])
                            act(guzU[:, i, :], psF[:, bank, 0:256], AF.Gelu_apprx_tanh, [("pf", bank)], [("guzU", i // 8)])
                            act(guzZ[:, i, :], psF[:, bank, 256:512], AF.Gelu_apprx_tanh, [("pf", bank)], [("guzZ", i // 8)])
                        def gm_batch(bt):
                            L = []
                            i0 = bt * NB
                            st_ = bt % NSET
                            sqb, zc, zc2, mb, zn, gmn = sqb_[st_], zc_[st_], zc2_[st_], mb_[st_], zn_[st_], gmn_[st_]
                            K = lambda n: (n, st_)
                            zfl = guzZ[:, i0:i0 + NB, :]
                            zv = zfl.rearrange("p b (g c) -> p (b g) c", c=64)
                            kz = [("guzZ", i0 // 8)]
                            ku = [("guzU", i0 // 8)]
                            zc3 = zc[:].rearrange("p b (g c) -> p (b g) c", c=64)
                            zc23 = zc2[:].rearrange("p b (g c) -> p (b g) c", c=64)
                            mbk = st_
                            tb = bt % 2
                            toff = ((bt // 2) % 2) * 512
                            L.append(lambda: S.op("dve", lambda e, o=msum[:, bt, :], a=zv: e.tensor_reduce(out=o, in_=a, axis=AX.X, op=ALU.add),
                                                  reads=kz, writes=[("msum", bt)]))
                            L.append(lambda: act(sqb[:], zfl, AF.Square, kz, [K("sqb")]))
                            L.append(lambda: S.op("dve", lambda e, o=qsum[:, bt, :], a=sqb[:].rearrange("p b (g c) -> p (b g) c", c=64):
                                                  e.tensor_reduce(out=o, in_=a, axis=AX.X, op=ALU.add), reads=[K("sqb")], writes=[("qsum", bt)]))
                            L.append(lambda: ts1("dve", mean[:, bt, :], msum[:, bt, :], 1.0 / 64, ALU.mult, [("msum", bt)], [("mean", bt)]))
                            L.append(lambda: tt("dve", m2[:, bt, :], mean[:, bt, :], mean[:, bt, :], ALU.mult, [("mean", bt)], [("m2", bt)]))
                            L.append(lambda: stt("dve", var[:, bt, :], qsum[:, bt, :], 1.0 / 64, m2[:, bt, :], ALU.mult, ALU.subtract,
                                                 [("qsum", bt), ("m2", bt)], [("var", bt)]))
                            L.append(lambda: act(lnv[:, bt, :], var[:, bt, :], AF.Ln, [("var", bt)], [("lnv", bt)], bias=EPS))
                            L.append(lambda: act(rsl[:, bt, :], lnv[:, bt, :], AF.Exp, [("lnv", bt)], [("rsl", bt)], scale=-0.5))
                            L.append(lambda: tt("dve", zc3, zv, mean[:, bt, :].unsqueeze(2).broadcast_to([128, 4 * NB, 64]), ALU.subtract,
                                                kz + [("mean", bt)], [K("zc")]))
                            L.append(lambda: tt("dve", zc23, zc3, rsl[:, bt, :].unsqueeze(2).broadcast_to([128, 4 * NB, 64]), ALU.mult,
                                                [K("zc"), ("rsl", bt)], [K("zc2")]))
                            L.append(lambda: tt("dve", zc[:], zc2[:], lng[:, :].unsqueeze(1).broadcast_to([128, NB, 256]), ALU.mult,
                                                [K("zc2"), ("lng",)], [K("zc")]))
                            L.append(lambda: tt("dve", zn[:], zc[:], lnb[:, :].unsqueeze(1).broadcast_to([128, NB, 256]), ALU.add,
                                                [K("zc"), ("lnb",)], [K("zn")]))

                            def mix_mms():
                                for j in range(NB):
                                    bk = mbk
                                    for g in range(4):
                                        c0_ = j * 256 + g * 64
                                        mm(psF[:, bk, c0_:c0_ + 64], wsT[:, g, :], zn[:, j, g * 64:(g + 1) * 64], True, True,
                                           [("wsT",), K("zn")], [("pf", bk)])
                            L.append(mix_mms)
                            L.append(lambda: tt("dve", mb[:].rearrange("p b (g c) -> p b g c", c=64),
                                                psF[:, mbk, :].rearrange("p (h g c) -> p h g c", h=NB, c=64),
                                                bsT[:, :].unsqueeze(1).unsqueeze(3).broadcast_to([128, NB, 4, 64]), ALU.add,
                                                [("pf", mbk), ("bsT",)], [K("mb")]))
                            L.append(lambda: tt("dve", mb[:], mb[:], guzU[:, i0:i0 + NB, :], ALU.mult, [K("mb")] + ku, [K("mb")]))
                            L.append(lambda: act(sqb[:], mb[:], AF.Square, [K("mb")], [K("sqb")]))
                            L.append(lambda: S.op("dve", lambda e, o=ssg[:, bt, :], a=sqb[:]: e.tensor_reduce(out=o, in_=a, axis=AX.X, op=ALU.add),
                                                  reads=[K("sqb")], writes=[("ssg", bt)]))
                            L.append(lambda: rstd_chain(ssg[:, bt, :], lgg[:, bt, :], rg[:, bt, :], 256.0, ("ssg", bt), ("lgg", bt), ("rg", bt)))
                            L.append(lambda: tt("dve", zc[:], mb[:], rg[:, bt, :].unsqueeze(2).broadcast_to([128, NB, 256]), ALU.mult,
                                                [K("mb"), ("rg", bt)], [K("zc")]))
                            L.append(lambda: tt("dve", gmn[:], zc[:], ggb[:, :].unsqueeze(1).broadcast_to([128, NB, 256]), ALU.mult,
                                                [K("zc"), ("ggb",)], [K("gmn")]))

                            def trs():
                                for j in range(NB):
                                    for c in range(2):
                                        q_ = j * 2 + c
                                        tr(psT[:, tb, toff + q_ * 128: toff + (q_ + 1) * 128], gmn[:, j, c * 128:(c + 1) * 128], ident[:],
                                           [K("gmn"), ("ident",)], [("pf", 6 + tb)])
                            L.append(trs)
                            L.append(lambda: cp("act" if bt % 2 == 0 else "dve",
                                                gmT[:, :, i0 * 128:(i0 + NB) * 128].rearrange("p c (j i) -> p c j i", i=128),
                                                psT[:, tb, toff:toff + NB * 256].rearrange("p (j c i) -> p c j i", c=2, i=128),
                                                [("pf", 6 + tb)], [("gmT", i0 // 4)]))
                            return L

                        for bp in range(0, NBT, NSET):
                            Ls = [gm_batch(bp + k_) for k_ in range(NSET)]
                            for fs in zip(*Ls):
                                for f_ in fs:
                                    f_()
                    S.barrier()
                    with contextlib.ExitStack() as ST:
                        wqkv = [alloc(ST, "wqkv%d" % i, [128, 3, 8, 128], BF16) for i in range(2)]
                        wt = [alloc(ST, "wt%d" % i, [128, 2, 3, 256], BF16) for i in range(2)]
                        kT = alloc(ST, "kT", [128, 4096], BF16)
                        vT = alloc(ST, "vT", [128, 4096], BF16)
                        qT = alloc(ST, "qT", [128, 2048], BF16)
                        vaug = [alloc(ST, "vaug%d" % i, [128, 32, 192], BF16) for i in range(2)]
                        acc = [alloc(ST, "acc%d" % i, [128, 2048], F32) for i in range(2)]
                        rec = alloc(ST, "rec", [128, 1, 512], F32)
                        lnd = alloc(ST, "lnd", [128, 1, 512], F32)
                        P2 = [alloc(ST, "P2_%d" % i, [128, 2, 512], BF16) for i in range(4)]
                        otmp = [alloc(ST, "otmp%d" % i, [128, 512], F32) for i in range(2)]

                        def load_pair_weights(p):
                            wb = p % 2
                            for k, col0 in enumerate((p * 128, 768 + p * 128, 1536 + p * 128)):
                                dma("pool", wqkv[wb][:, k], w_in_v[:, :, col0:col0 + 128], [], [("wqkv", wb, k)])
                            dma("pool", wt[wb][:], wtab_d[:, 2 * p:2 * p + 2], [], [("wt", wb)])

                        for vbi in range(2):
                            if s == 0:
                                cp("pool", vaug[vbi][:, 0:16, 64:128], hm[:, :].unsqueeze(1).broadcast_to([128, 16, 64]),
                                   [("hm",)], [("vaug1", vbi)])
                            else:
                                memset("pool", vaug[vbi][:, 0:16, 64:128], 1.0, [("vaug1", vbi)])
                            memset("pool", vaug[vbi][:, 16:32, 64:128], 1.0, [("vaug1", vbi)])
                        load_pair_weights(0)
                        pat_ctr = 0
                        bank_rr = [0]
                        tb_rr = [0]

                        def proj_bank():
                            bank_rr[0] += 1
                            return bank_rr[0] % 6

                        for p in range(DBG['pairs']):
                            wb = p % 2
                            if p + 1 < 6:
                                load_pair_weights(p + 1)
                            if s == 0 and p < 4:
                                precast_ffn(p)
                            if p == 1:
                                dma("pool", wo[:], w_out_v, [], [("wo",)])
                            ktiles = range(8) if s == 0 else range(4, 8)
                            if s == 1:
                                dma("sp", kT[:, 0:2048], kvs[p, 0], [("kvs", p, 0)], [("kT", t_) for t_ in range(4)])
                                dma("sp", vT[:, 0:2048], kvs[p, 1], [("kvs", p, 1)], [("vT", t_) for t_ in range(4)])
                            for (k, dst, name, tiles) in ((1, kT, "kT", ktiles), (2, vT, "vT", ktiles), (0, qT, "qT", range(4, 8))):
                                for t in tiles:
                                    bank = proj_bank()
                                    hk = [("hnT", 4 * t + j) for j in range(4)]
                                    for c in range(8):
                                        mm(psF[:, bank, :], wqkv[wb][:, k, c, :], hnT[:, c, t * 512:(t + 1) * 512], c == 0, c == 7,
                                           hk + [("wqkv", wb, k)], [("pf", bank)])
                                    if k == 0:
                                        tq = t - 4
                                        if evac_eng() == "act":
                                            act(qT[:, tq * 512:(tq + 1) * 512], psF[:, bank, :], AF.Copy, [("pf", bank)], [("qT", tq)], scale=0.125)
                                        else:
                                            ts1("dve", qT[:, tq * 512:(tq + 1) * 512], psF[:, bank, :], 0.125, ALU.mult, [("pf", bank)], [("qT", tq)])
                                    else:
                                        cp(evac_eng(), dst[:, t * 512:(t + 1) * 512], psF[:, bank, :], [("pf", bank)], [(name, t)])
                                if s == 0 and k in (1, 2):
                                    dma("sp", kvs[p, k - 1], dst[:, 2048:4096], [(name, t_) for t_ in range(4, 8)], [("kvs", p, k - 1)])
                            for pi, d in enumerate(PATTERNS):
                                nbd = 32 // d
                                hb = nbd // 2
                                vbi = pat_ctr % 2
                                pat_ctr += 1
                                vb = vaug[vbi]
                                vv = vT[:].rearrange("p (b i d) -> p d b i", d=d, i=128)
                                kv = kT[:].rearrange("p (b i d) -> p d b i", d=d, i=128)
                                qv = qT[:].rearrange("p (b i d) -> p d b i", d=d, i=128)

                                def tiles_of(r, b, d=d):
                                    lo = (b * 128 * d) // 512
                                    hi = (b * 128 * d + 127 * d + r) // 512
                                    return range(lo, hi + 1)

                                if d == 1:
                                    halo_groups = [[15]]
                                elif d == 4:
                                    halo_groups = [[3, 7, 11, 15]]
                                else:
                                    halo_groups = [list(range(0, 8)), list(range(8, 16))]
                                own_groups = [list(range(16, 24)), list(range(24, 32))]
                                vb4 = vb[:].rearrange("p n (a c) -> p n a c", c=64)
                                for grp in halo_groups + own_groups:
                                    tb = tb_rr[0] % 2
                                    tb_rr[0] += 1
                                    for j, idv in enumerate(grp):
                                        r, b = vid_inv(d, idv)
                                        tr(psT[:, tb, j * 128:(j + 1) * 128], vv[:, r, b, :], ident[:],
                                           [("vT", t) for t in tiles_of(r, b)] + [("ident",)], [("pf", 6 + tb)])
                                    n = len(grp)
                                    step = (grp[1] - grp[0]) if n > 1 else 1
                                    o_ap = vb4[:, grp[0]:grp[0] + step * (n - 1) + 1:step, 0:3:2, :]
                                    i_ap = psT[:, tb, 0:n * 128].rearrange("p (n a c) -> p n a c", a=2, c=64)
                                    wk = [("vaug", vbi, idv) for idv in grp]
                                    if s == 0 and grp[0] < 16:
                                        act(o_ap, i_ap, AF.Copy, [("pf", 6 + tb), ("hm",)], wk, scale=hm[:, 0:1])
                                    else:
                                        cp(evac_eng(), o_ap, i_ap, [("pf", 6 + tb)], wk)

                                if d == 1:
                                    groups = [[(0, 16 + 4 * g + j) for j in range(4)] for g in range(4)]
                                elif d == 4:
                                    groups = [[(r, 4 + j) for j in range(4)] for r in range(4)]
                                else:
                                    groups = [[(4 * g + j, 1) for j in range(4)] for g in range(4)]
                                units = [(gi, half) for gi in range(4) for half in range(2)]
                                ustate = {}
                                pu_rr = [0]
                                pt_rr = [0]
                                og_rr = [0]

                                def emit_qk(u, d=d, hb=hb, kv=kv, qv=qv, groups=groups, pi=pi, wb=wb):
                                    gi, half = units[u]
                                    sA = 2 * (pu_rr[0] % 3)
                                    pu_rr[0] += 1
                                    pt = pt_rr[0] % 4
                                    pt_rr[0] += 1
                                    for jj in range(2):
                                        r, b = groups[gi][2 * half + jj]
                                        qk = [("qT", t - 4) for t in tiles_of(r, b)]
                                        for which, kb_ in enumerate((b - 1, b)):
                                            kk = [("kT", t) for t in tiles_of(r, kb_)]
                                            col = jj * 256 + which * 128
                                            for hh in range(2):
                                                psl = slice(0, 64) if hh == 0 else slice(64, 128)
                                                mm(psF[:, sA + hh, col:col + 128], kv[psl, r, kb_, :], qv[psl, r, b - hb, :], True, True,
                                                   kk + qk, [("pf", sA + hh)])
                                    act(P2[pt][:], psF[:, sA:sA + 2, :], AF.Exp, [("pf", sA), ("pf", sA + 1)], [("P2", pt)])
                                    pv4 = P2[pt][:].rearrange("p h (j k) -> p h j k", k=256)
                                    tt("dve", pv4, pv4, wt[wb][:, :, pi, :].unsqueeze(2).broadcast_to([128, 2, 2, 256]), ALU.mult,
                                       [("P2", pt), ("wt", wb)], [("P2", pt)])
                                    ustate[u] = pt

                                def emit_pv(u, d=d, groups=groups, vb=vb, vbi=vbi, pi=pi):
                                    gi, half = units[u]
                                    pt = ustate.pop(u)
                                    ob = 6
                                    for jj in range(2):
                                        r, b = groups[gi][2 * half + jj]
                                        q = 2 * half + jj
                                        for which, kb_ in enumerate((b - 1, b)):
                                            idk = vid(d, r, kb_)
                                            for hh in range(2):
                                                vcols = slice(0, 128) if hh == 0 else slice(64, 192)
                                                mm(psF[:, ob + hh, q * 128:(q + 1) * 128], vb[:, idk, vcols],
                                                   P2[pt][:, hh, jj * 256 + which * 128: jj * 256 + (which + 1) * 128],
                                                   which == 0, which == 1,
                                                   [("vaug", vbi, idk), ("vaug1", vbi), ("P2", pt)], [("pf", ob + hh)])
                                    if half == 1:
                                        for hh in range(2):
                                            accv = acc[hh][:].rearrange("p (b i d) -> p d b i", d=d, i=128)
                                            if d == 1:
                                                dst = accv[:, 0, 4 * gi:4 * gi + 4, :]
                                            elif d == 4:
                                                dst = accv[:, gi, 0:4, :]
                                            else:
                                                dst = accv[:, 4 * gi:4 * gi + 4, 0, :]
                                            src = psF[:, ob + hh, :].rearrange("p (q i) -> p q i", i=128)
                                            if pi == 0:
                                                cp("act", dst, src, [("pf", ob + hh)], [("acc", hh)])
                                            else:
                                                oi_ = og_rr[0] % 2
                                                og_rr[0] += 1
                                                cp("act", otmp[oi_][:].rearrange("p (q i) -> p q i", i=128), src, [("pf", ob + hh)], [("otmp", oi_)])
                                                tt("pool", dst, dst, otmp[oi_][:].rearrange("p (q i) -> p q i", i=128), ALU.add,
                                                   [("otmp", oi_), ("acc", hh)], [("acc", hh)])

                                SKEW = 2
                                nu = len(units)
                                for u in range(min(SKEW, nu)):
                                    emit_qk(u)
                                for u in range(nu):
                                    if u + SKEW < nu:
                                        emit_qk(u + SKEW)
                                    emit_pv(u)
                            for hh in range(2):
                                nps = slice(0, 64) if hh == 0 else slice(64, 128)
                                dps = slice(64, 128) if hh == 0 else slice(0, 64)
                                for ch in range(4):
                                    cols = slice(ch * 512, (ch + 1) * 512)
                                    rb = 0
                                    act(lnd[dps, rb, :], acc[hh][dps, cols], AF.Ln, [("acc", hh)], [("lnd", hh, rb)])
                                    act(rec[nps, rb, :], lnd[dps, rb, :], AF.Exp, [("lnd", hh, rb)], [("rec", hh, rb)], scale=-1.0)
                                    stt("dve", attnT[nps, p, cols], acc[hh][nps, cols], gacol[nps, p:p + 1], rec[nps, rb, :],
                                        ALU.mult, ALU.mult, [("acc", hh), ("gacol",), ("rec", hh, rb)], [("attnT", p, hh, ch)])
                    S.barrier()
                S.barrier()
                with contextlib.ExitStack() as SC:
                    NT = DBG['tilesC'] if DBG['phaseC'] else 0
                    g2bc = alloc(SC, "g2bc", [128, 1024], F32)
                    gfbc = alloc(SC, "gfbc", [128, 1024], F32)
                    ones_bb = alloc(SC, "ones_bb", [128, 128], BF16)
                    sq = alloc(SC, "sq", [128, 6, 512], BF16)
                    attnN = alloc(SC, "attnN", [128, 6, 512], BF16)
                    lnra = alloc(SC, "lnra", [128, 512], F32)
                    rabc = alloc(SC, "rabc", [128, 512], F32)
                    h1 = [alloc(SC, "h1_%d" % i, [128, 4, 1024], F32) for i in range(2)]
                    hn2 = [alloc(SC, "hn2_%d" % i, [128, 1024], BF16) for i in range(2)]
                    hn2T = [alloc(SC, "hn2T%d" % i, [128, 8, 512], BF16) for i in range(2)]
                    aT = alloc(SC, "aT", [128, 32, 512], BF16)
                    rl = [alloc(SC, "rl%d" % i, [128, 512], F32) for i in range(3)]
                    W1b = [alloc(SC, "W1b%d" % i, [128, 8, 512], BF16) for i in range(2)]
                    W2b = [alloc(SC, "W2b%d" % i, [128, 4, 512], BF16) for i in range(3)]
                    ot = [alloc(SC, "ot%d" % i, [128, 1024], F32) for i in range(2)]
                    junkc = alloc(SC, "junkc", [128, 1024], BF16)
                    ss2 = alloc(SC, "ss2", [128, 16], F32)
                    ln2 = alloc(SC, "ln2", [128, 16], F32)
                    r2 = alloc(SC, "r2", [128, 16], F32)
                    ssf = alloc(SC, "ssf", [128, 16, 2], F32)
                    ssft = alloc(SC, "ssft", [128, 16], F32)
                    lnf = alloc(SC, "lnf", [128, 16], F32)
                    rf = alloc(SC, "rf", [128, 16], F32)
                    dma("sp", g2bc[:], g2_d, [], [("g2bc",)])
                    dma("sp", gfbc[:], gf_d, [], [("gfbc",)])
                    memset("pool", ones_bb[:], 1.0, [("ones_bb",)])
                    cst = dict(w1=0, w2=0, rl=0, ot=0)
                    w1_slot = {}
                    w2_slot = {}
                    units2 = [(h, g) for h in range(2) for g in range(8)]

                    def issue_w1(t, g):
                        sl = cst["w1"] % 2
                        cst["w1"] += 1
                        dma("sp", W1b[sl][:], w1s[:, g], [("w1s", g)], [("W1b", sl)])
                        w1_slot[(t, g)] = sl

                    def issue_w2(t, u):
                        h, g = units2[u]
                        sl = cst["w2"] % 3
                        cst["w2"] += 1
                        dma("sp", W2b[sl][:], w2s[:, h, g], [("w2s", h, g)], [("W2b", sl)])
                        w2_slot[(t, u)] = sl

                    def akeys(t):
                        return [("attnT", p_, h_, t) for p_ in range(6) for h_ in range(2)]

                    wprep_done = set()

                    def Wprep_c(t, bank):
                        if t in wprep_done:
                            return
                        wprep_done.add(t)
                        T0 = t * 512
                        eng_ = "dve" if t < 2 else "pool"
                        tt(eng_, sq[:], attnT[:, :, T0:T0 + 512], attnT[:, :, T0:T0 + 512], ALU.mult, akeys(t), [("sq",)])
                        for c in range(6):
                            mm(psF[:, bank, :], ones_bb[:], sq[:, c, :], c == 0, c == 5, [("sq",), ("ones_bb",)], [("pf", bank)])
                        act(lnra[:], psF[:, bank, :], AF.Ln, [("pf", bank)], [("lnra",)], scale=1.0 / 768, bias=EPS)
                        act(rabc[:], lnra[:], AF.Exp, [("lnra",)], [("rabc",)], scale=-0.5)
                        tt(eng_, attnN[:], attnT[:, :, T0:T0 + 512], rabc[:, :].unsqueeze(1).broadcast_to([128, 6, 512]), ALU.mult,
                           akeys(t) + [("rabc",)], [("attnN",)])

                    def Wprep(t):
                        T0 = t * 512
                        Wprep_c(t, 0)
                        for blk in range(4):
                            row0 = s * 2048 + T0 + blk * 128
                            dma("sp", h1[t % 2][:, blk, :], x_own[row0:row0 + 128, :], [], [("h1", t % 2, blk)])

                    def WP1(t, blk):
                        c0 = t * 512 + blk * 128
                        bb = 2 * (blk % 2)
                        for half in range(2):
                            for c in range(8):
                                lhs = attnN[:, c, blk * 128:(blk + 1) * 128] if c < 6 else gmT[:, c - 6, c0:c0 + 128]
                                mm(psF[:, bb + half, :], lhs, wo[:, c, half * 512:(half + 1) * 512], c == 0, c == 7,
                                   [("attnN",), ("gmT", t), ("wo",)], [("pf", bb + half)])
                        hv = h1[t % 2][:, blk, :].rearrange("p (h n) -> p h n", n=512)
                        tt("dve", hv, hv, psF[:, bb:bb + 2, :], ALU.add,
                           [("pf", bb), ("pf", bb + 1), ("h1", t % 2, blk)], [("h1", t % 2, blk)])

                    def WA(t, blk):
                        bi = t * 4 + blk
                        act(junkc[:], h1[t % 2][:, blk, :], AF.Square, [("h1", t % 2, blk)], [("junkc",), ("ss2", bi)],
                            accum_out=ss2[:, bi:bi + 1])
                        rstd_chain(ss2[:, bi:bi + 1], ln2[:, bi:bi + 1], r2[:, bi:bi + 1], 1024.0, ("ss2", bi), ("ln2", bi), ("r2", bi))
                        hb_ = blk % 2
                        stt("dve", hn2[hb_][:], h1[t % 2][:, blk, :], r2[:, bi:bi + 1], g2bc[:], ALU.mult, ALU.mult,
                            [("h1", t % 2, blk), ("r2", bi), ("g2bc",)], [("hn2", hb_)])

                    def WP2(t, blk):
                        hb_ = blk % 2
                        tb = blk % 2
                        for c in range(8):
                            tr(psT[:, tb, c * 128:(c + 1) * 128], hn2[hb_][:, c * 128:(c + 1) * 128], ident[:],
                               [("hn2", hb_), ("ident",)], pTkeys(tb))
                        cp("dve", hn2T[t % 2][:, :, blk * 128:(blk + 1) * 128],
                           psT[:, tb, :].rearrange("p (c i) -> p c i", i=128), pTkeys(tb), [("hn2T", t % 2, blk)])

                    def ff1_group(t, g):
                        hkeys = [("hn2T", t % 2, b_) for b_ in range(4)]
                        cur = w1_slot[(t, g)]
                        for cc in range(4):
                            ffc = 4 * g + cc
                            bank = 4 + ffc % 2
                            for c in range(8):
                                mm(psF[:, bank, :], W1b[cur][:, c, cc * 128:(cc + 1) * 128], hn2T[t % 2][:, c, :], c == 0, c == 7,
                                   hkeys + [("W1b", cur)], [("pf", bank)])
                            ri = cst["rl"] % 3
                            cst["rl"] += 1
                            act(rl[ri][:], psF[:, bank, :], AF.Relu, [("pf", bank)], [("rl", ri)])
                            tt("pool", aT[:, ffc, :], rl[ri][:], rl[ri][:], ALU.mult, [("rl", ri)], [("aT", ffc)])

                    def ff2(t):
                        for u, (h, g) in enumerate(units2):
                            if u + 2 < len(units2):
                                issue_w2(t, u + 2)
                            if u == 12 and t + 1 < NT:
                                issue_w1(t + 1, 0)
                            if u == 3 and t + 2 < NT:
                                Wprep_c(t + 2, 4)
                            sl = w2_slot[(t, u)]
                            for blk in range(4):
                                for cc in range(4):
                                    ffc = 4 * g + cc
                                    mm(psF[:, blk, :], aT[:, ffc, blk * 128:(blk + 1) * 128], W2b[sl][:, cc, :],
                                       g == 0 and cc == 0, g == 7 and cc == 3,
                                       [("aT", ffc), ("W2b", sl)], [("pf", blk)])
                            if g == 7:
                                for blk in range(4):
                                    bi = t * 4 + blk
                                    hsl = h1[t % 2][:, blk, h * 512:(h + 1) * 512]
                                    tt("dve", hsl, hsl, psF[:, blk, :], ALU.add, [("pf", blk), ("h1", t % 2, blk)], [("h1", t % 2, blk)])
                                    act(junkc[:, 0:512], hsl, AF.Square, [("h1", t % 2, blk)],
                                        [("junkc",), ("ssf", bi, h)], accum_out=ssf[:, bi, h:h + 1])

                    def final(t):
                        for blk in range(4):
                            bi = t * 4 + blk
                            tt("dve", ssft[:, bi:bi + 1], ssf[:, bi, 0:1], ssf[:, bi, 1:2], ALU.add,
                               [("ssf", bi, 0), ("ssf", bi, 1)], [("ssft", bi)])
                            rstd_chain(ssft[:, bi:bi + 1], lnf[:, bi:bi + 1], rf[:, bi:bi + 1], 1024.0, ("ssft", bi), ("lnf", bi), ("rf", bi))
                            oi = cst["ot"] % 2
                            cst["ot"] += 1
                            stt("dve", ot[oi][:], h1[t % 2][:, blk, :], rf[:, bi:bi + 1], gfbc[:], ALU.mult, ALU.mult,
                                [("h1", t % 2, blk), ("rf", bi), ("gfbc",)], [("ot", oi)])
                            row0 = s * 2048 + t * 512 + blk * 128
                            out_ops.append(dma("sp", out_d[row0:row0 + 128, :], ot[oi][:], [("ot", oi)], [("out", row0)]))

                    if NT > 0:
                        issue_w1(0, 0)
                        Wprep(0)
                        WP1(0, 0)
                        WP1(0, 1)
                        WA(0, 0)
                        WA(0, 1)
                        WP2(0, 0)
                        WP1(0, 2)
                        WP2(0, 1)
                        WP1(0, 3)
                        WA(0, 2)
                        WA(0, 3)
                        WP2(0, 2)
                        WP2(0, 3)
                    for t in range(NT):
                        nxt = t + 1 < NT
                        for g in range(8):
                            if g + 1 < 8:
                                issue_w1(t, g + 1)
                            if g == 6:
                                issue_w2(t, 0)
                                issue_w2(t, 1)
                            if nxt:
                                if g == 0:
                                    Wprep(t + 1)
                                if g < 4:
                                    WP1(t + 1, g)
                                if 1 <= g < 5:
                                    WA(t + 1, g - 1)
                                if 2 <= g < 6:
                                    WP2(t + 1, g - 2)
                            ff1_group(t, g)
                        ff2(t)
                        final(t)
                S.barrier()
        S.emit(final_wait_ops=out_ops)
    return nc, S


_CACHE = {}


def _host_consts():
    if "c" not in _CACHE:
        _CACHE["c"] = dict(
            wtab=make_wtab(),
            tri=np.triu(np.ones((128, 128), dtype=np.float32)),
            ident=np.eye(128, dtype=np.float32),
        )
    return _CACHE["c"]


def kernel(x, norm1_g, w_in, sgu_ln_g, sgu_ln_b, sgu_w, sgu_b, attn_out_g, gmlp_out_g,
           w_out, norm2_g, w_ff1, w_ff2, final_norm_g):
    f = lambda a: np.ascontiguousarray(np.asarray(a, dtype=np.float32))
    x = f(x)
    consts = _host_consts()
    bc = lambda v, n: np.ascontiguousarray(np.broadcast_to(f(v).reshape(1, n), (128, n)))
    shared = dict(
        w_in=f(w_in)[0], w_out=f(w_out)[0], w_ff1=f(w_ff1)[0], w_ff2=f(w_ff2)[0],
        g1_bc=bc(np.asarray(norm1_g)[0], 1024), g2_bc=bc(np.asarray(norm2_g)[0], 1024), gf_bc=bc(final_norm_g, 1024),
        ga_col=np.ascontiguousarray(f(attn_out_g)[0].reshape(6, 128).T),
        gg_bc=bc(np.asarray(gmlp_out_g)[0], 256),
        lng_bc=bc(np.asarray(sgu_ln_g)[0].reshape(-1), 256), lnb_bc=bc(np.asarray(sgu_ln_b)[0].reshape(-1), 256),
        bsT=np.ascontiguousarray(f(sgu_b)[0].T),
        wsT=np.ascontiguousarray(f(sgu_w)[0].transpose(2, 0, 1)),
        tri=consts["tri"], wtab=consts["wtab"], ident=consts["ident"],
    )
    in_maps = []
    for c in range(N_CORES):
        b, half = divmod(c, 2)
        m = dict(shared)
        m["x_own"] = np.ascontiguousarray(x[b, half * 4096:(half + 1) * 4096])
        if half == 1:
            m["x_halo"] = np.ascontiguousarray(x[b, 2048:4096])
            m["hmask"] = np.ones((128, 64), dtype=np.float32)
        else:
            m["x_halo"] = np.zeros((2048, 1024), dtype=np.float32)
            m["hmask"] = np.zeros((128, 64), dtype=np.float32)
        in_maps.append(m)
    if "nc" not in _CACHE:
        _CACHE["nc"] = build_program()
    nc, _ = _CACHE["nc"]
    res = run_bass_kernel_spmd(nc, in_maps, core_ids=list(range(N_CORES)))
    outp = np.empty((4, 8192, 1024), dtype=np.float32)
    for c in range(N_CORES):
        b, half = divmod(c, 2)
        outp[b, half * 4096:(half + 1) * 4096] = res.results[c]["out"]
    return outp
```
